# Optimizing a Trainium2 kernel written in Bass

```python
import math
import jax, jax.numpy as jnp
from jax import lax
import numpy as np

D_MODEL = 1024
BATCH = 2
SEQ = 8192
DEPTH = 2

M_HEADS = 4
M_QK_DIM = 128
M_V_DIM = 256
M_QK_WIDTH = M_HEADS * M_QK_DIM
M_V_WIDTH = M_HEADS * M_V_DIM
M_CHUNK = 64
CONV_K = 4
A_Q_HEADS = 16
A_KV_HEADS = 4
A_HEAD_DIM = 64
A_GROUP = A_Q_HEADS // A_KV_HEADS
A_Q_WIDTH = A_Q_HEADS * A_HEAD_DIM
A_KV_WIDTH = A_KV_HEADS * A_HEAD_DIM
WINDOW = 128
N_BUCKETS = 32
MAX_DISTANCE = 128
D_FF = 4 * D_MODEL
EPS = 1e-6

IN_SIZES = (2 * M_QK_WIDTH, M_V_WIDTH, M_V_WIDTH, M_HEADS, M_HEADS,
            A_Q_WIDTH, A_KV_WIDTH, A_KV_WIDTH, D_MODEL, D_MODEL)
N_IN = sum(IN_SIZES)

kernel_name = "hybrid_mlstm_swa_sink_t5bias_sqrelu"


def rmsnorm(x, g):
    xf = x.astype(jnp.float32)
    y = xf * lax.rsqrt(jnp.mean(xf * xf, axis=-1, keepdims=True) + EPS)
    return (y * g.astype(jnp.float32)).astype(x.dtype)


def t5_bucket(n):
    max_exact = N_BUCKETS // 2
    n = np.maximum(n, 0)
    large = max_exact + (np.log(np.maximum(n, 1) / max_exact)
                         / np.log(MAX_DISTANCE / max_exact)
                         * (N_BUCKETS - max_exact)).astype(np.int32)
    large = np.minimum(large, N_BUCKETS - 1)
    return np.where(n < max_exact, n, large).astype(np.int32)


def causal_conv(x, w, b):
    S = x.shape[1]
    xp = jnp.pad(x, ((0, 0), (CONV_K - 1, 0), (0, 0)))
    y = b
    for j in range(CONV_K):
        y = y + xp[:, j:j + S] * w[j]
    return y


def mlstm_chunkwise(q, k, v, ig, logf):
    B, S, H, dk = q.shape
    dv = v.shape[-1]
    L = M_CHUNK
    NC = S // L

    def chunks4(t):
        return t.reshape(B, NC, L, H, t.shape[-1]).transpose(1, 0, 3, 2, 4)

    def chunks3(t):
        return t.reshape(B, NC, L, H).transpose(1, 0, 3, 2)

    tril = jnp.tril(jnp.ones((L, L), dtype=bool))

    def step(carry, xs):
        C, n, m = carry
        qc, kc, vc, ic, fc = xs
        b = jnp.cumsum(fc, axis=-1)
        dmat = jnp.where(tril, b[..., :, None] - b[..., None, :] + ic[..., None, :], -jnp.inf)
        m_inter = b + m[..., None]
        m_comb = jnp.maximum(m_inter, jnp.max(dmat, axis=-1))
        w_intra = jnp.exp(dmat - m_comb[..., None])
        w_inter = jnp.exp(m_inter - m_comb)
        s = jnp.einsum('bhld,bhsd->bhls', qc, kc) * w_intra
        num = (jnp.einsum('bhls,bhsv->bhlv', s, vc)
               + w_inter[..., None] * jnp.einsum('bhld,bhdv->bhlv', qc, C))
        den = jnp.sum(s, axis=-1) + w_inter * jnp.einsum('bhld,bhd->bhl', qc, n)
        h = num / jnp.maximum(jnp.abs(den), jnp.exp(-m_comb))[..., None]
        bL = b[..., -1]
        g = bL[..., None] - b + ic
        m_new = jnp.maximum(bL + m, jnp.max(g, axis=-1))
        decay = jnp.exp(bL + m - m_new)
        wk = jnp.exp(g - m_new[..., None])
        C_new = decay[..., None, None] * C + jnp.einsum('bhl,bhld,bhlv->bhdv', wk, kc, vc)
        n_new = decay[..., None] * n + jnp.einsum('bhl,bhld->bhd', wk, kc)
        return (C_new, n_new, m_new), h

    init = (jnp.zeros((B, H, dk, dv), jnp.float32),
            jnp.zeros((B, H, dk), jnp.float32),
            jnp.zeros((B, H), jnp.float32))
    _, hs = lax.scan(step, init, (chunks4(q), chunks4(k), chunks4(v), chunks3(ig), chunks3(logf)))
    return hs.transpose(1, 0, 3, 2, 4).reshape(B, S, H, dv)


def sliding_window_attention(q, k, v, sinks, rel_bias):
    B, S = q.shape[:2]
    W = WINDOW
    NB = S // W
    qb = q.astype(jnp.float32).reshape(B, NB, W, A_KV_HEADS, A_GROUP, A_HEAD_DIM)
    kb = k.astype(jnp.float32).reshape(B, NB, W, A_KV_HEADS, A_HEAD_DIM)
    vb = v.astype(jnp.float32).reshape(B, NB, W, A_KV_HEADS, A_HEAD_DIM)

    def with_prev(t):
        prev = jnp.concatenate([jnp.zeros_like(t[:, :1]), t[:, :-1]], axis=1)
        return jnp.concatenate([prev, t], axis=2)

    k_ctx, v_ctx = with_prev(kb), with_prev(vb)
    scores = jnp.einsum('bnqhgd,bnkhd->bnhgqk', qb, k_ctx) * (A_HEAD_DIM ** -0.5)

    dist = np.arange(W)[:, None] + W - np.arange(2 * W)[None, :]
    bias = rel_bias.astype(jnp.float32)[t5_bucket(dist)]
    bias = bias.transpose(2, 0, 1).reshape(A_KV_HEADS, A_GROUP, W, 2 * W)
    valid_rel = jnp.asarray((dist >= 0) & (dist < W))
    kpos = jnp.arange(NB)[:, None, None] * W - W + jnp.arange(2 * W)[None, None, :]
    valid = valid_rel[None] & (kpos >= 0)
    logits = jnp.where(valid[None, :, None, None], scores + bias, -jnp.inf)

    sink = sinks.astype(jnp.float32).reshape(A_KV_HEADS, A_GROUP, 1)
    m = jnp.maximum(jnp.max(logits, axis=-1), sink)
    p = jnp.exp(logits - m[..., None])
    denom = jnp.sum(p, axis=-1) + jnp.exp(sink - m)
    out = jnp.einsum('bnhgqk,bnkhd->bnqhgd', p / denom[..., None], v_ctx)
    return out.reshape(B, S, A_Q_WIDTH).astype(q.dtype)


def mixer_block(h, w_in, conv_w, conv_b, b_igate, b_fgate, mlstm_norm_g, sinks,
                rel_bias, w_branch_m, w_branch_a, w_out):
    B, S, _ = h.shape
    proj = h @ w_in
    offsets = [int(o) for o in np.cumsum(IN_SIZES)[:-1]]
    mqk, mv, mo, mi, mf, aq, ak, av, ga, gb = jnp.split(proj, offsets, axis=-1)

    mqk = jax.nn.silu(causal_conv(mqk, conv_w, conv_b))
    mq, mk = jnp.split(mqk.astype(jnp.float32), 2, axis=-1)
    mq = mq.reshape(B, S, M_HEADS, M_QK_DIM) * (M_QK_DIM ** -0.5)
    mk = mk.reshape(B, S, M_HEADS, M_QK_DIM)
    mvf = mv.astype(jnp.float32).reshape(B, S, M_HEADS, M_V_DIM)
    ig = mi.astype(jnp.float32) + b_igate.astype(jnp.float32)
    logf = jax.nn.log_sigmoid(mf.astype(jnp.float32) + b_fgate.astype(jnp.float32))
    hm = mlstm_chunkwise(mq, mk, mvf, ig, logf)
    hm = hm * lax.rsqrt(jnp.mean(hm * hm, axis=-1, keepdims=True) + EPS)
    hm = hm * mlstm_norm_g.astype(jnp.float32).reshape(M_HEADS, M_V_DIM)
    hm = (hm.reshape(B, S, M_V_WIDTH) * jax.nn.sigmoid(mo.astype(jnp.float32))).astype(h.dtype)

    ha = sliding_window_attention(aq.reshape(B, S, A_Q_HEADS, A_HEAD_DIM),
                                  ak.reshape(B, S, A_KV_HEADS, A_HEAD_DIM),
                                  av.reshape(B, S, A_KV_HEADS, A_HEAD_DIM),
                                  sinks, rel_bias)

    y = jax.nn.sigmoid(ga) * (hm @ w_branch_m) + jax.nn.sigmoid(gb) * (ha @ w_branch_a)
    return y @ w_out


def sqrelu_mlp(h, w_up, w_down):
    return jnp.square(jax.nn.relu(h @ w_up)) @ w_down


def setup_inputs(seed: int = 0) -> dict:
    key = jax.random.key(seed)
    ks = jax.random.split(key, 20)
    f32 = jnp.float32
    nrm = lambda k, shape, s: jax.random.normal(k, shape, f32) * s
    return {
        "x": nrm(ks[0], (BATCH, SEQ, D_MODEL), 1.0),
        "norm_mix_g": 1.0 + nrm(ks[1], (DEPTH, D_MODEL), 0.05),
        "w_in": nrm(ks[2], (DEPTH, D_MODEL, N_IN), D_MODEL ** -0.5),
        "conv_w": nrm(ks[3], (DEPTH, CONV_K, 2 * M_QK_WIDTH), CONV_K ** -0.5),
        "conv_b": nrm(ks[4], (DEPTH, 2 * M_QK_WIDTH), 0.02),
        "b_igate": nrm(ks[5], (DEPTH, M_HEADS), 0.1),
        "b_fgate": jnp.linspace(3.0, 6.0, M_HEADS, dtype=f32)[None] + nrm(ks[6], (DEPTH, M_HEADS), 0.1),
        "mlstm_norm_g": 1.0 + nrm(ks[7], (DEPTH, M_V_WIDTH), 0.05),
        "attn_sinks": nrm(ks[8], (DEPTH, A_Q_HEADS), 0.5),
        "rel_bias": nrm(ks[9], (N_BUCKETS, A_Q_HEADS), 0.5),
        "w_branch_m": nrm(ks[10], (DEPTH, M_V_WIDTH, D_MODEL), M_V_WIDTH ** -0.5),
        "w_branch_a": nrm(ks[11], (DEPTH, A_Q_WIDTH, D_MODEL), A_Q_WIDTH ** -0.5),
        "w_out": nrm(ks[12], (DEPTH, D_MODEL, D_MODEL), D_MODEL ** -0.5),
        "norm_mlp_g": 1.0 + nrm(ks[13], (DEPTH, D_MODEL), 0.05),
        "w_up": nrm(ks[14], (DEPTH, D_MODEL, D_FF), D_MODEL ** -0.5),
        "w_down": nrm(ks[15], (DEPTH, D_FF, D_MODEL), D_FF ** -0.5),
        "final_norm_g": 1.0 + nrm(ks[16], (D_MODEL,), 0.05),
    }


def reference(x, norm_mix_g, w_in, conv_w, conv_b, b_igate, b_fgate, mlstm_norm_g,
              attn_sinks, rel_bias, w_branch_m, w_branch_a, w_out, norm_mlp_g,
              w_up, w_down, final_norm_g):
    for l in range(DEPTH):
        h = rmsnorm(x, norm_mix_g[l])
        x = x + mixer_block(h, w_in[l], conv_w[l], conv_b[l], b_igate[l], b_fgate[l],
                            mlstm_norm_g[l], attn_sinks[l], rel_bias,
                            w_branch_m[l], w_branch_a[l], w_out[l])
        h = rmsnorm(x, norm_mlp_g[l])
        x = x + sqrelu_mlp(h, w_up[l], w_down[l])
    return rmsnorm(x, final_norm_g)
```

```python
import types
import numpy as np
from contextlib import ExitStack
import concourse.bass as bass
import concourse.mybir as mybir
from concourse.bass_utils import run_bass_kernel_spmd

F32 = mybir.dt.float32
BF16 = mybir.dt.bfloat16
AF = mybir.ActivationFunctionType
ALU = mybir.AluOpType

D = 1024
SEQ = 8192
NCORES = 8
TL = 2048
NTL = 16
NTB = 8
TB = 1024
NBLK = 2
NIN = 6664
DFF = 4096
EPS = 1e-6
OFF_MQ, OFF_MK, OFF_MV, OFF_MO, OFF_MI, OFF_AQ, OFF_AK, OFF_AV, OFF_GA, OFF_GB = (
    0, 512, 1024, 2048, 3072, 3080, 4104, 4360, 4616, 5640)
QSCALE = 128 ** -0.5
CW = 260
XW = 4 * CW + D
NEG = -30000.0

ENGS = ("pe", "act", "dve", "pool", "sp")
CUT = [0]


def _snap(fn):
    if fn is None or fn.__closure__ is None:
        return fn
    cells = []
    for c in fn.__closure__:
        try:
            cells.append(types.CellType(c.cell_contents))
        except ValueError:
            cells.append(c)
    g = types.FunctionType(fn.__code__, fn.__globals__, fn.__name__, fn.__defaults__, tuple(cells))
    g.__kwdefaults__ = fn.__kwdefaults__
    return g


class Prog:
    def __init__(self, nc):
        self.nc = nc
        self.ops = {e: [] for e in ENGS}
        self.count = {e: 0 for e in ENGS}
        self.waited = {e: {} for e in ENGS}
        self.last_write = {}
        self.readers = {}
        self.dma_count = {}
        self.cc_keys = []
        self.sem_handles = {}

    def _deps(self, reads, writes):
        deps = []
        for r in reads:
            if r in self.last_write:
                deps.append(self.last_write[r])
        for w in writes:
            if w in self.last_write:
                deps.append(self.last_write[w])
            deps.extend(self.readers.get(w, ()))
        return deps

    def _commit(self, token, reads, writes):
        for r in reads:
            self.readers.setdefault(r, []).append(token)
        for w in writes:
            self.last_write[w] = token
            self.readers[w] = []

    def _waits(self, eng, deps):
        need = {}
        for (k, v) in deps:
            if k == "pe" and eng == "pe":
                continue
            if self.waited[eng].get(k, 0) >= v:
                continue
            need[k] = max(need.get(k, 0), v)
        for k, v in need.items():
            self.waited[eng][k] = v
        return list(need.items())

    def op(self, eng, fn, reads=(), writes=()):
        psr = [r for r in reads if r.startswith("PA") or r.startswith("PT")]
        if psr:
            reads = [r for r in reads if r not in psr]
            writes = list(writes) + [r for r in psr if r not in writes]
        waits = self._waits(eng, self._deps(reads, writes))
        self.count[eng] += 1
        token = (eng, self.count[eng])
        self.ops[eng].append(("op", _snap(fn), waits, None))
        self._commit(token, reads, writes)
        return token

    def dma(self, eng, fn, group, reads=(), writes=()):
        waits = self._waits(eng, self._deps(reads, writes))
        key = "dma:" + group
        self.dma_count[key] = self.dma_count.get(key, 0) + 1
        token = (key, 16 * self.dma_count[key])
        self.ops[eng].append(("dma", _snap(fn), waits, key))
        self._commit(token, reads, writes)
        return token

    def cc(self, fn, name, reads=(), writes=()):
        alld = [(k, 16 * c) for k, c in self.dma_count.items()]
        alle = [(e, self.count[e]) for e in ("pe", "act", "dve", "pool") if self.count[e] > 0]
        waits = self._waits("pool", self._deps(reads, writes) + alld + alle)
        key = "cc:all"
        if key not in self.cc_keys:
            self.cc_keys.append(key)
        self.cc_n = getattr(self, "cc_n", 0) + 1
        token = (key, self.cc_n)
        self.ops["pool"].append(("cc", _snap(fn), waits, key))
        for e in ENGS:
            self.ops[e].append(("wait", None, [token], None))
            self.waited[e][key] = self.cc_n
        self._commit(token, reads, writes)
        return token

    def barrier(self):
        comp = ("pe", "act", "dve")
        toks = [(e, self.count[e]) for e in comp if self.count[e] > 0]
        for e in comp:
            need = []
            for (k, v) in toks:
                if self.waited[e].get(k, 0) >= v:
                    continue
                self.waited[e][k] = v
                need.append((k, v))
            if need:
                self.ops[e].append(("wait", None, need, None))

    def final_wait(self, eng, tokens):
        waits = self._waits(eng, tokens)
        self.ops[eng].append(("wait", None, waits, None))

    def emit(self, stack):
        nc = self.nc
        keys = list(ENGS) + sorted(self.dma_count.keys()) + self.cc_keys
        for k in keys:
            self.sem_handles[k] = stack.enter_context(
                nc.semaphore("s_" + k.replace(":", "_").replace(".", "_")))
        block = stack.enter_context(nc.Block())
        sh = self.sem_handles

        def make(engname):
            def body(e):
                for (kind, fn, waits, key) in self.ops[engname]:
                    for (k, v) in waits:
                        e.wait_ge(sh[k], v)
                    if kind == "op":
                        fn(e).then_inc(sh[engname], 1)
                    elif kind == "dma":
                        fn(e).then_inc(sh[key], 16)
                    elif kind == "cc":
                        fn(e).then_inc(sh[key])
            return body

        block.tensor(make("pe"))
        block.scalar(make("act"))
        block.vector(make("dve"))
        block.gpsimd(make("pool"))
        block.sync(make("sp"))


def build_program(depth=2, dbg=False, nsteps=None, no_cc=False, l0=0, final=None, mode="fused"):
    final = (not dbg) if final is None else final
    LAYERS = list(range(l0, l0 + depth))
    nc = bass.Bass("TRN2", target_bir_lowering=False)

    def din(name, shape, dt=F32):
        return nc.dram_tensor(name, list(shape), dt, kind="ExternalInput").ap()

    x_d = din("x", [TL, D])
    xh_d = din("xh", [128, D])
    flags_d = din("flags", [128, 24])
    w_in_d = {l: din(f"w_in{l}", [D, NIN]) for l in LAYERS}
    w_m_d = {l: din(f"w_branch_m{l}", [D, D]) for l in LAYERS}
    w_a_d = {l: din(f"w_branch_a{l}", [D, D]) for l in LAYERS}
    w_o_d = {l: din(f"w_out{l}", [D, D]) for l in LAYERS}
    w_up_d = {l: din(f"w_up{l}", [D, DFF]) for l in LAYERS}
    w_dn_d = {l: din(f"w_down{l}", [DFF, D]) for l in LAYERS}
    gcols_d = din("gcols", [128, 4, 8])
    gfin_d = din("gfin", [128, D])
    convw_d = din("convw", [128, 2, 8, 4])
    convb_d = din("convb", [128, 2, 8])
    gateb_d = din("gateb", [128, 2, 64])
    gm_d = din("gm", [128, 2, D])
    sinks_d = din("sinks", [128, 2, 16])
    biasT_d = din("biasT", [128, 16, 2, 128])
    maskT_d = din("maskT", [128, 2, 128])
    if mode == "sum":
        cst_out = nc.dram_tensor("cst_out", [128, 4 * CW], F32, kind="ExternalOutput").ap()
    else:
        out_d = nc.dram_tensor("out", [TL, D], F32, kind="ExternalOutput").ap()
    if mode == "main":
        gath_in = din("gath_in", [NCORES * 128, 4 * CW])
    if dbg:
        d_hT = nc.dram_tensor("d_hT", [128, 8 * 9 * 128], BF16, kind="ExternalOutput").ap()
        d_hmT = nc.dram_tensor("d_hmT", [128, 8 * TB], BF16, kind="ExternalOutput").ap()
        d_haT = nc.dram_tensor("d_haT", [128, 8 * TB], BF16, kind="ExternalOutput").ap()
        d_misc = nc.dram_tensor("d_misc", [128, 4 * CW + 7 * 32], F32, kind="ExternalOutput").ap()
    bounce = [nc.dram_tensor(f"bounce{l}", [128, 4 * CW], F32) for l in range(2)]
    gath = [nc.dram_tensor(f"gath{l}", [NCORES * 128, 4 * CW], F32) for l in range(2)]
    bounce_h = nc.dram_tensor("bounce_h", [128, D], F32)
    gath_h = nc.dram_tensor("gath_h", [NCORES * 128, D], F32)

    st = ExitStack()
    with st:
        uid = [0]

        def sb(name, shape, dt=F32):
            uid[0] += 1
            return st.enter_context(nc.sbuf_tensor(f"sb_{name}_{uid[0]}", list(shape), dt))

        def ps(name, shape, dt=F32):
            uid[0] += 1
            return st.enter_context(nc.psum_tensor(f"ps_{name}_{uid[0]}", list(shape), dt))

        P = Prog(nc)
        x = sb("x", [128, NTL, D])
        xhalo = sb("xhalo", [128, D])
        hT = sb("hT", [128, 8, 9 * 128], BF16)
        hTh = sb("hTh", [128, 8, 128], BF16)
        hmT = sb("hmT", [128, 8, TB], BF16)
        haT = sb("haT", [128, 8, TB], BF16)
        NWB = 3
        wb = [sb(f"wb{i}", [128, 8, 512], BF16) for i in range(NWB)]
        ident = sb("ident", [128, 128], BF16)
        onesf = sb("onesf", [128, 128])
        trif = sb("trif", [128, 128])
        maskS = sb("maskS", [128, 128])
        biasm = sb("biasm", [128, 16, 2, 128], BF16)
        gcols = sb("gcols", [128, 4, 8])
        gfin = sb("gfin", [128, D])
        convw = sb("convw", [128, 2, 8, 4])
        convb = sb("convb", [128, 2, 8])
        gateb = sb("gateb", [128, 2, 64])
        gm = sb("gm", [128, D])
        sinks = sb("sinks", [128, 2, 16])
        esink = sb("esink", [128, 16])
        flags = sb("flags", [128, 24])
        Cst = sb("Cst", [128, 4, CW])
        Cbf = sb("Cbf", [128, 4, 257], BF16)
        gsum = sb("gsum", [128, 8, 8])
        spt = sb("spt", [128, 32])
        a_t = sb("a_t", [128, 32])
        ea = sb("ea", [128, 32])
        dtmp = sb("dtmp", [128, 32])
        ksc = sb("ksc", [128, 32])
        ebL = sb("ebL", [128, 32])
        ss = sb("ss", [128, 8])
        junk = sb("junk", [128, D], BF16)
        xn = sb("xn", [128, D], BF16)
        scr = sb("scr", [128, 2304])
        PA = [ps(f"PA{i}", [128, 512]) for i in range(6)]
        PT = [ps(f"PT{i}", [128, 1024], BF16) for i in range(2)]
        rot = {"pa": 0, "pt": 0, "wb": 0}

        def next_pa(n=4):
            i = rot["pa"] % n
            rot["pa"] += 1
            return i

        def next_pt():
            i = rot["pt"] % 2
            rot["pt"] += 1
            return i

        def load_w(parts):
            slot = rot["wb"] % NWB
            rot["wb"] += 1
            for (src, dst_fn) in parts:
                P.dma("pool", (lambda e, s=src, d=dst_fn(wb[slot]): e.dma_start(out=d, in_=s)),
                      f"wb{slot}", writes=[f"wb{slot}"])
            return slot

        def wcols(src2d, c0, n, dst0):
            s = src2d[:, c0:c0 + n].rearrange("(k p) n -> p k n", p=128)
            return (s, lambda w, a=dst0, b=n: w[:, :, a:a + b])

        def cload(dst, src, name):
            P.dma("sp", lambda e: e.dma_start(out=dst, in_=src), name, writes=[name])

        cload(gcols[:], gcols_d, "gcols")
        cload(gfin[:], gfin_d, "gfin")
        cload(convw[:], convw_d, "convw")
        cload(convb[:], convb_d, "convb")
        cload(gateb[:], gateb_d, "gateb")
        cload(sinks[:], sinks_d, "sinks")
        cload(flags[:], flags_d, "flags")
        cload(xhalo[:], xh_d, "xhalo")
        cload(maskS[:], maskT_d[:, 0, :], "maskS")
        cload(trif[:], maskT_d[:, 1, :], "trif")
        for hh in range(2):
            P.dma("sp", lambda e, hh=hh: e.dma_start(
                out=scr[:, 0:2048].rearrange("p (h b q) -> p h b q", h=8, b=2),
                in_=biasT_d[:, hh * 8:(hh + 1) * 8, :, :]), "scr", writes=["scr"])
            for h8 in range(8):
                for blk, mt in ((0, maskS), (1, trif)):
                    P.op("dve", lambda e, hh=hh, h8=h8, blk=blk, mt=mt: e.tensor_tensor(
                        out=biasm[:, hh * 8 + h8, blk, :],
                        in0=scr[:, (h8 * 2 + blk) * 128:(h8 * 2 + blk + 1) * 128],
                        in1=mt[:], op=ALU.add),
                        reads=["scr", "maskS", "trif"], writes=["biasm"])
        P.op("pool", lambda e: e.memset(onesf[:], 1.0), writes=["onesf"])
        P.op("pool", lambda e: e.affine_select(out=ident[:], in_=onesf[:], pattern=[[1, 128]], base=0,
                                                 channel_multiplier=-1, compare_op=ALU.is_equal, fill=0.0),
             reads=["onesf"], writes=["ident"])
        P.op("pool", lambda e: e.affine_select(out=trif[:], in_=onesf[:], pattern=[[1, 128]], base=0,
                                                 channel_multiplier=-1, compare_op=ALU.is_ge, fill=0.0),
             reads=["onesf", "biasm"], writes=["trif"])
        P.op("dve", lambda e: e.tensor_scalar(out=maskS[:], in0=trif[:], scalar1=QSCALE, scalar2=None,
                                               op0=ALU.mult),
             reads=["trif", "biasm"], writes=["maskS"])
        for i in range(NTL):
            P.dma("sp", lambda e, i=i: e.dma_start(out=x[:, i, :], in_=x_d[i * 128:(i + 1) * 128, :]),
                  "xin", writes=[f"x{i}"])
        tok_x = P.last_write[f"x{NTL - 1}"]
        for i in range(NTL):
            P.last_write[f"x{i}"] = tok_x

        def rstd_from_ss(col, inv_n):
            c = ss[:, col:col + 1]
            P.op("dve", lambda e: e.tensor_scalar(out=c, in0=c, scalar1=inv_n, scalar2=EPS,
                                                   op0=ALU.mult, op1=ALU.add), reads=["ss"], writes=["ss"])
            P.op("act", lambda e: e.activation(out=c, in_=c, func=AF.Sqrt), reads=["ss"], writes=["ss"])
            P.op("dve", lambda e: e.reciprocal(out=c, in_=c), reads=["ss"], writes=["ss"])

        def norm_to_hT(xsrc, xres, n_idx, slot):
            P.op("act", lambda e: e.activation(out=junk[:], in_=xsrc, func=AF.Square, accum_out=ss[:, 0:1]),
                 reads=[xres], writes=["junk", "ss"])
            rstd_from_ss(0, 1.0 / D)
            P.op("dve", lambda e: e.tensor_scalar(out=xn[:], in0=xsrc, scalar1=ss[:, 0:1], scalar2=None,
                                                   op0=ALU.mult), reads=[xres, "ss"], writes=["xn"])
            pt = next_pt()
            for c in range(8):
                P.op("pe", lambda e, c=c: e.transpose(out=PT[pt][:, c * 128:(c + 1) * 128],
                                                       in_=xn[:, c * 128:(c + 1) * 128], identity=ident[:]),
                     reads=["xn", "ident"], writes=[f"PT{pt}"])
            for c in range(8):
                eng = "act" if c % 2 == 0 else "dve"
                o = hT[:, c, slot * 128:(slot + 1) * 128]
                i_ = PT[pt][:, c * 128:(c + 1) * 128]
                g = gcols[:, n_idx, c:c + 1]
                if eng == "act":
                    P.op("act", lambda e, o=o, i_=i_, g=g: e.activation(out=o, in_=i_, func=AF.Copy, scale=g),
                         reads=[f"PT{pt}", "gcols"], writes=[f"hT{slot}"])
                else:
                    P.op("dve", lambda e, o=o, i_=i_, g=g: e.tensor_scalar(out=o, in0=i_, scalar1=g,
                                                                           scalar2=None, op0=ALU.mult),
                         reads=[f"PT{pt}", "gcols"], writes=[f"hT{slot}"])

        HT_ALL = [f"hT{s}" for s in range(9)]
        HT_BLK = [f"hT{s}" for s in range(1, 9)]

        def proj_fm(wslot, wc0, ncol, t0, tn, evac):
            pa = next_pa()
            for k in range(8):
                P.op("pe", lambda e, k=k: e.matmul(PA[pa][0:ncol, 0:tn], lhsT=wb[wslot][:, k, wc0:wc0 + ncol],
                                                    rhs=hT[:, k, t0:t0 + tn], start=(k == 0), stop=(k == 7)),
                     reads=[f"wb{wslot}"] + HT_ALL, writes=[f"PA{pa}"])
            evac(pa)

        def proj_tm(wslot, wc0, ncol, slot, evac):
            pa = next_pa()
            for k in range(8):
                P.op("pe", lambda e, k=k: e.matmul(PA[pa][:, 0:ncol], lhsT=hT[:, k, slot * 128:(slot + 1) * 128],
                                                    rhs=wb[wslot][:, k, wc0:wc0 + ncol], start=(k == 0), stop=(k == 7)),
                     reads=[f"wb{wslot}", f"hT{slot}"], writes=[f"PA{pa}"])
            evac(pa)

        def phase_norm(l, b, n_idx, with_halo=True):
            if with_halo:
                if b == 0:
                    norm_to_hT(xhalo[:], "xhalo", n_idx, 0)
                else:
                    P.op("dve", lambda e: e.tensor_copy(out=hT[:, :, 0:128], in_=hTh[:]),
                         reads=["hTh"], writes=["hT0"])
            for i in range(NTB):
                gt_ = b * NTB + i
                norm_to_hT(x[:, gt_, :], f"x{gt_}", n_idx, i + 1)
            if with_halo and b == 0:
                P.op("dve", lambda e: e.tensor_copy(out=hTh[:], in_=hT[:, :, 8 * 128:9 * 128]),
                     reads=["hT8"], writes=["hTh"])

        def phase_gates(l, b):
            wslot = load_w([wcols(w_in_d[l], OFF_MI, 8, 0)])
            G = PA[5]
            for i in range(NTB):
                for k in range(8):
                    P.op("pe", lambda e, i=i, k=k: e.matmul(G[:, i * 8:(i + 1) * 8],
                                                            lhsT=hT[:, k, (i + 1) * 128:(i + 2) * 128],
                                                            rhs=wb[wslot][:, k, 0:8], start=(k == 0), stop=(k == 7)),
                         reads=[f"wb{wslot}", f"hT{i + 1}"], writes=["PA5"])
            if CUT[0] == 1:
                return
            P.op("dve", lambda e: e.tensor_tensor(out=gsum[:].rearrange("p i c -> p (i c)"), in0=G[:, 0:64],
                                                   in1=gateb[:, l, :], op=ALU.add),
                 reads=["PA5", "gateb"], writes=["gsum"])
            if CUT[0] == 2:
                return
            spv = spt[:].rearrange("p (i h) -> p i h", h=4)
            P.op("act", lambda e: e.activation(out=spv, in_=gsum[:, :, 4:8], func=AF.Exp, scale=-1.0),
                 reads=["gsum"], writes=["spt"])
            P.op("act", lambda e: e.activation(out=spt[:], in_=spt[:], func=AF.Ln, bias=1.0),
                 reads=["spt"], writes=["spt"])
            if CUT[0] == 3:
                return
            P.op("pe", lambda e: e.matmul(G[:, 64:96], lhsT=trif[:], rhs=spt[:], start=True, stop=True),
                 reads=["trif", "spt", "gsum"], writes=["PA5"])
            P.op("pe", lambda e: e.matmul(G[:, 96:128], lhsT=onesf[:], rhs=spt[:], start=True, stop=True),
                 reads=["onesf", "spt"], writes=["PA5"])
            if CUT[0] == 4:
                return
            P.op("dve", lambda e: e.tensor_tensor(out=a_t[:].rearrange("p (i h) -> p i h", h=4), in0=gsum[:, :, 0:4],
                                                   in1=G[:, 64:96].rearrange("p (i h) -> p i h", h=4), op=ALU.add),
                 reads=["PA5", "gsum"], writes=["a_t"])
            P.op("act", lambda e: e.activation(out=ea[:], in_=a_t[:], func=AF.Exp), reads=["a_t"], writes=["ea"])
            if CUT[0] == 5:
                return
            P.op("dve", lambda e: e.tensor_tensor(out=dtmp[:], in0=a_t[:], in1=G[:, 96:128], op=ALU.subtract),
                 reads=["PA5", "a_t"], writes=["dtmp"])
            P.op("act", lambda e: e.activation(out=ksc[:], in_=dtmp[:], func=AF.Exp), reads=["dtmp"], writes=["ksc"])
            if CUT[0] == 6:
                return
            P.op("act", lambda e: e.activation(out=ebL[:], in_=G[:, 96:128], func=AF.Exp, scale=-1.0),
                 reads=["PA5"], writes=["ebL"])

        def conv_silu(pa_evac_src, l, chunk, dst):
            acc = scr[:, 1152:1152 + TB]
            pre = scr
            w = lambda j: convw[:, l, chunk, j:j + 1]
            P.op("dve", lambda e: e.tensor_scalar(out=acc, in0=pre[:, 128:128 + TB], scalar1=w(3),
                                                   scalar2=convb[:, l, chunk:chunk + 1], op0=ALU.mult, op1=ALU.add),
                 reads=["scrA", "convw", "convb"], writes=["scrB"])
            for j in range(3):
                P.op("dve", lambda e, j=j: e.scalar_tensor_tensor(out=acc, in0=pre[:, 125 + j:125 + j + TB],
                                                                  scalar=w(j), in1=acc, op0=ALU.mult, op1=ALU.add),
                     reads=["scrA", "scrB"], writes=["scrB"])
            P.op("act", lambda e: e.activation(out=dst, in_=acc, func=AF.Silu), reads=["scrB"], writes=["qk"])

        def phase_mlstm_head(l, b, h, sweep):
            with ExitStack() as ph:
                def psb(name, shape, dt=F32):
                    uid[0] += 1
                    return ph.enter_context(nc.sbuf_tensor(f"ph_{name}_{uid[0]}", list(shape), dt))
                kT = psb("kT", [128, TB], BF16)
                ktok = psb("ktok", [128, NTB, 128], BF16)
                vaug = psb("vaug", [128, NTB, 257], BF16)
                if sweep == 2:
                    qT = psb("qT", [128, TB], BF16)
                    osig = psb("osig", [128, NTB, 256], BF16)
                    Lt = [psb(f"Lt{i}", [128, 128]) for i in range(2)]
                    Et = [psb(f"Et{i}", [128, 128]) for i in range(2)]
                    EM = [psb(f"EM{i}", [128, 128]) for i in range(2)]
                    qp = [psb(f"qp{i}", [128, 128], BF16) for i in range(2)]
                    Sb = [psb(f"Sb{i}", [128, 128], BF16) for i in range(2)]
                    hmt = [psb(f"hmt{i}", [128, 256], BF16) for i in range(2)]
                    t1 = [psb(f"t1{i}", [128, 256]) for i in range(2)]
                    Cb2 = [psb(f"Cb2{i}", [128, 257], BF16) for i in range(2)]
                    nrm = psb("nrm", [128, 2, 4])
                if sweep == 2:
                    ws = load_w([wcols(w_in_d[l], OFF_MQ + h * 128, 128, 0),
                                 wcols(w_in_d[l], OFF_MK + h * 128, 128, 128),
                                 wcols(w_in_d[l], OFF_MV + h * 256, 256, 256)])
                    ws2 = load_w([wcols(w_in_d[l], OFF_MO + h * 256, 256, 0)])
                    kc0 = 128
                else:
                    ws = load_w([wcols(w_in_d[l], OFF_MK + h * 128, 128, 128),
                                 wcols(w_in_d[l], OFF_MV + h * 256, 256, 256)])
                    kc0 = 128
                P.op("dve", lambda e: e.memset(vaug[:, :, 256:257], 1.0), writes=["vaug"])

                def pre_evac(tg):
                    def f(pa):
                        P.op("act", lambda e: e.activation(out=scr[:, tg * 384:(tg + 1) * 384], in_=PA[pa][:, 0:384],
                                                            func=AF.Copy), reads=[f"PA{pa}"], writes=["scrA"])
                    return f
                if sweep == 2:
                    for tg in range(3):
                        proj_fm(ws, 0, 128, tg * 384, 384, pre_evac(tg))
                    conv_silu(None, l, h, qT[:])
                for tg in range(3):
                    proj_fm(ws, kc0, 128, tg * 384, 384, pre_evac(tg))
                conv_silu(None, l, 4 + h, kT[:])
                for i in range(NTB):
                    def vev(pa, i=i):
                        P.op("act", lambda e: e.activation(out=vaug[:, i, 0:256], in_=PA[pa][:, 0:256], func=AF.Copy),
                             reads=[f"PA{pa}"], writes=["vaug"])
                    proj_tm(ws, 256, 256, i + 1, vev)
                if sweep == 2:
                    for i in range(NTB):
                        def oev(pa, i=i):
                            P.op("act", lambda e: e.activation(out=osig[:, i, :], in_=PA[pa][:, 0:256], func=AF.Sigmoid),
                                 reads=[f"PA{pa}"], writes=["osig"])
                        proj_tm(ws2, 0, 256, i + 1, oev)
                for i in range(NTB):
                    pt = next_pt()
                    col = i * 4 + h
                    P.op("pe", lambda e, i=i, pt=pt: e.transpose(out=PT[pt][:, 0:128], in_=kT[:, i * 128:(i + 1) * 128],
                                                                  identity=ident[:]),
                         reads=["qk", "ident"], writes=[f"PT{pt}"])
                    P.op("dve", lambda e, i=i, pt=pt, col=col: e.tensor_scalar(
                        out=ktok[:, i, :], in0=PT[pt][:, 0:128], scalar1=ksc[:, col:col + 1], scalar2=None, op0=ALU.mult),
                        reads=[f"PT{pt}", "ksc"], writes=["ktok"])
                def updU(i):
                    pu = 2 + i % 2
                    P.op("pe", lambda e: e.matmul(PA[pu][:, 0:257], lhsT=ktok[:, i, :], rhs=vaug[:, i, :], start=True, stop=True),
                         reads=["ktok", "vaug"], writes=[f"PA{pu}"])

                def updC(i, last):
                    col = i * 4 + h
                    pu = 2 + i % 2
                    P.op("dve", lambda e: e.scalar_tensor_tensor(out=Cst[:, h, 0:257], in0=Cst[:, h, 0:257],
                                                                 scalar=ebL[:, col:col + 1], in1=PA[pu][:, 0:257],
                                                                 op0=ALU.mult, op1=ALU.add),
                         reads=[f"PA{pu}", "ebL", "Cst"], writes=["Cst"])
                    if sweep == 1:
                        P.op("dve", lambda e: e.tensor_scalar(out=Cst[:, h, 257:258], in0=Cst[:, h, 257:258],
                                                               scalar1=ebL[:, col:col + 1], scalar2=None, op0=ALU.mult),
                             reads=["ebL", "Cst"], writes=["Cst"])
                    elif not last:
                        v = (i + 1) % 2
                        P.op("act", lambda e: e.activation(out=Cb2[v][:], in_=Cst[:, h, 0:257], func=AF.Copy),
                             reads=["Cst"], writes=[f"Cb2{v}"])

                def front(i):
                    col = i * 4 + h
                    u = i % 2
                    pb = (0, 4)[u]
                    tsl = slice(i * 128, (i + 1) * 128)
                    P.op("dve", lambda e: e.tensor_scalar(out=Lt[u][:], in0=onesf[:], scalar1=spt[:, col:col + 1],
                                                           scalar2=None, op0=ALU.mult),
                         reads=["onesf", "spt"], writes=[f"Lt{u}"])
                    P.op("pe", lambda e: e.matmul(PA[pb][:, 128:256], lhsT=kT[:, tsl], rhs=qT[:, tsl], start=True, stop=True),
                         reads=["qk"], writes=[f"PA{pb}"])
                    P.op("pe", lambda e: e.matmul(PA[pb][:, 0:128], lhsT=Lt[u][:], rhs=trif[:], start=True, stop=True),
                         reads=[f"Lt{u}", "trif"], writes=[f"PA{pb}"])
                    P.op("act", lambda e: e.activation(out=Et[u][:], in_=PA[pb][:, 0:128], func=AF.Exp, scale=-1.0),
                         reads=[f"PA{pb}"], writes=[f"Et{u}"])
                    P.op("dve", lambda e: e.tensor_tensor(out=EM[u][:], in0=Et[u][:], in1=maskS[:], op=ALU.mult),
                         reads=[f"Et{u}", "maskS"], writes=[f"EM{u}"])
                    P.op("dve", lambda e: e.scalar_tensor_tensor(out=qp[u][:], in0=qT[:, tsl], scalar=QSCALE,
                                                                 in1=Et[u][:], op0=ALU.mult, op1=ALU.mult),
                         reads=["qk", f"Et{u}"], writes=[f"qp{u}"])
                    P.op("dve", lambda e: e.scalar_tensor_tensor(out=Sb[u][:], in0=PA[pb][:, 128:256],
                                                                 scalar=ea[:, col:col + 1], in1=EM[u][:],
                                                                 op0=ALU.mult, op1=ALU.mult),
                         reads=[f"PA{pb}", "ea", f"EM{u}"], writes=[f"Sb{u}"])

                def numer(i):
                    u = i % 2
                    pn = (1, 5)[u]
                    P.op("pe", lambda e: e.matmul(PA[pn][:, 0:257], lhsT=Sb[u][:], rhs=vaug[:, i, :], start=True, stop=False),
                         reads=[f"Sb{u}", "vaug"], writes=[f"PA{pn}"])
                    P.op("pe", lambda e: e.matmul(PA[pn][:, 0:257], lhsT=qp[u][:], rhs=Cb2[u][:], start=False, stop=True),
                         reads=[f"qp{u}", f"Cb2{u}"], writes=[f"PA{pn}"])

                def back(i):
                    u = i % 2
                    pn = (1, 5)[u]
                    tsl = slice(i * 128, (i + 1) * 128)
                    r_ = nrm[:, u, 0:1]
                    s2 = nrm[:, u, 1:2]
                    a_ = nrm[:, u, 2:3]
                    P.op("dve", lambda e: e.tensor_scalar(out=a_, in0=PA[pn][:, 256:257], scalar1=-1.0, scalar2=1.0,
                                                           op0=ALU.mult, op1=ALU.max), reads=[f"PA{pn}"], writes=[f"nrm{u}"])
                    P.op("dve", lambda e: e.scalar_tensor_tensor(out=r_, in0=PA[pn][:, 256:257], scalar=1.0, in1=a_,
                                                                 op0=ALU.max, op1=ALU.max),
                         reads=[f"PA{pn}", f"nrm{u}"], writes=[f"nrm{u}"])
                    P.op("dve", lambda e: e.reciprocal(out=r_, in_=r_), reads=[f"nrm{u}"], writes=[f"nrm{u}"])
                    P.op("act", lambda e: e.activation(out=junk[:, 0:256], in_=PA[pn][:, 0:256], func=AF.Square,
                                                        scale=r_, accum_out=s2),
                         reads=[f"PA{pn}", f"nrm{u}"], writes=["junk", f"nrmb{u}"])
                    P.op("act", lambda e: e.activation(out=s2, in_=s2, func=AF.Ln, scale=1.0 / 256, bias=EPS),
                         reads=[f"nrmb{u}"], writes=[f"nrmb{u}"])
                    P.op("act", lambda e: e.activation(out=s2, in_=s2, func=AF.Exp, scale=-0.5),
                         reads=[f"nrmb{u}"], writes=[f"nrmb{u}"])
                    P.op("dve", lambda e: e.tensor_tensor(out=s2, in0=s2, in1=r_, op=ALU.mult),
                         reads=[f"nrmb{u}", f"nrm{u}"], writes=[f"nrmb{u}"])
                    P.op("dve", lambda e: e.scalar_tensor_tensor(out=t1[u][:], in0=PA[pn][:, 0:256], scalar=s2,
                                                                 in1=gm[:, h * 256:(h + 1) * 256],
                                                                 op0=ALU.mult, op1=ALU.mult),
                         reads=[f"PA{pn}", f"nrmb{u}", "gm"], writes=[f"t1{u}"])
                    P.op("dve", lambda e: e.tensor_tensor(out=hmt[u][:], in0=t1[u][:], in1=osig[:, i, :], op=ALU.mult),
                         reads=[f"t1{u}", "osig"], writes=[f"hmt{u}"])
                    pt = next_pt()
                    for c in range(2):
                        P.op("pe", lambda e, c=c: e.transpose(out=PT[pt][:, c * 128:(c + 1) * 128],
                                                              in_=hmt[u][:, c * 128:(c + 1) * 128], identity=ident[:]),
                             reads=[f"hmt{u}", "ident"], writes=[f"PT{pt}"])
                    P.op("act", lambda e: e.activation(
                        out=hmT[:, 2 * h:2 * h + 2, tsl],
                        in_=PT[pt][:, 0:256].rearrange("p (c t) -> p c t", c=2), func=AF.Copy),
                        reads=[f"PT{pt}"], writes=["hmT"])

                if sweep == 1:
                    for i in range(NTB):
                        updU(i)
                        updC(i, i == NTB - 1)
                else:
                    P.op("act", lambda e: e.activation(out=Cb2[0][:], in_=Cst[:, h, 0:257], func=AF.Copy),
                         reads=["Cst"], writes=["Cb20"])
                    front(0)
                    for i in range(NTB):
                        if i + 1 < NTB:
                            front(i + 1)
                        updU(i)
                        numer(i)
                        updC(i, i == NTB - 1)
                        back(i)
                P.barrier()

        def phase_attn(l, b, g):
            with ExitStack() as ph:
                def psb(name, shape, dt=F32):
                    uid[0] += 1
                    return ph.enter_context(nc.sbuf_tensor(f"ph_{name}_{uid[0]}", list(shape), dt))
                qTa = psb("qTa", [128, 2, TB], BF16)
                kTa = psb("kTa", [128, 9 * 128], BF16)
                va = psb("va", [128, 9, 65], BF16)
                hat = [psb(f"hat{i}", [128, 4, 64], BF16) for i in range(2)]
                ND = 4
                lg = [psb(f"lg{i}", [128, 256]) for i in range(ND)]
                pTt = [psb(f"pTt{i}", [128, 2, 128], BF16) for i in range(ND)]
                dn = psb("dn", [128, 2, 4])
                ws = load_w([wcols(w_in_d[l], OFF_AQ + g * 256, 256, 0),
                             wcols(w_in_d[l], OFF_AK + g * 64, 64, 256),
                             wcols(w_in_d[l], OFF_AK + g * 64, 64, 320),
                             wcols(w_in_d[l], OFF_AV + g * 64, 64, 384)])
                for c in range(2):
                    for tg in range(2):
                        def qev(pa, c=c, tg=tg):
                            P.op("act", lambda e: e.activation(out=qTa[:, c, tg * 512:(tg + 1) * 512], in_=PA[pa][:, 0:512],
                                                                func=AF.Copy), reads=[f"PA{pa}"], writes=["qTa"])
                        proj_fm(ws, c * 128, 128, 128 + tg * 512, 512, qev)
                for tg in range(3):
                    def kev(pa, tg=tg):
                        P.op("dve", lambda e: e.tensor_copy(out=kTa[:, tg * 384:(tg + 1) * 384], in_=PA[pa][:, 0:384]),
                             reads=[f"PA{pa}"], writes=["kTa"])
                    proj_fm(ws, 256, 128, tg * 384, 384, kev)
                P.op("dve", lambda e: e.memset(va[:, :, 64:65], 1.0), writes=["va"])
                if b == 0:
                    P.op("dve", lambda e: e.tensor_copy(out=va[:, 0, 64:65], in_=flags[:, 16:17]),
                         reads=["flags"], writes=["va"])
                for s_ in range(9):
                    def vev(pa, s_=s_):
                        P.op("act", lambda e: e.activation(out=va[:, s_, 0:64], in_=PA[pa][:, 0:64], func=AF.Copy),
                             reads=[f"PA{pa}"], writes=["va"])
                    proj_tm(ws, 384, 64, s_, vev)
                items = [(i, j) for i in range(NTB) for j in range(4)]

                def stage_a(n):
                    i, j = items[n]
                    c = j // 2
                    p0 = (j % 2) * 64
                    pa = 2 + n % ND
                    tsl = slice(i * 128, (i + 1) * 128)
                    for blk in range(2):
                        P.op("pe", lambda e: e.matmul(
                            PA[pa][:, blk * 128:(blk + 1) * 128],
                            lhsT=kTa[p0:p0 + 64, (i + blk) * 128:(i + blk + 1) * 128],
                            rhs=qTa[p0:p0 + 64, c, tsl], start=True, stop=True),
                            reads=["kTa", "qTa"], writes=[f"PA{pa}"])

                def stage_b(n):
                    i, j = items[n]
                    hq = 4 * g + j
                    u = n % ND
                    pa = 2 + u
                    po = i % 2
                    P.op("dve", lambda e: e.scalar_tensor_tensor(
                        out=lg[u][:], in0=PA[pa][:, 0:256], scalar=0.125,
                        in1=biasm[:, hq, :, :].rearrange("p b q -> p (b q)"), op0=ALU.mult, op1=ALU.add),
                        reads=[f"PA{pa}", "biasm"], writes=[f"lg{u}"])
                    P.op("act", lambda e: e.activation(out=pTt[u][:].rearrange("p b q -> p (b q)"), in_=lg[u][:],
                                                        func=AF.Exp), reads=[f"lg{u}"], writes=[f"pTt{u}"])
                    P.op("pe", lambda e: e.matmul(PA[po][:, j * 65:(j + 1) * 65], lhsT=pTt[u][:, 0, :], rhs=va[:, i, :],
                                                  start=True, stop=False),
                         reads=[f"pTt{u}", "va"], writes=[f"PA{po}"])
                    P.op("pe", lambda e: e.matmul(PA[po][:, j * 65:(j + 1) * 65], lhsT=pTt[u][:, 1, :], rhs=va[:, i + 1, :],
                                                  start=False, stop=True),
                         reads=[f"pTt{u}", "va"], writes=[f"PA{po}"])
                    if j == 3:
                        ov = PA[po][:, 0:260].rearrange("p (h c) -> p h c", c=65)
                        dv = dn[:, po, :]
                        P.op("dve", lambda e: e.tensor_tensor(out=dv, in0=ov[:, :, 64], in1=esink[:, 4 * g:4 * g + 4], op=ALU.add),
                             reads=[f"PA{po}", "esink"], writes=[f"dn{po}"])
                        P.op("dve", lambda e: e.reciprocal(out=dv, in_=dv), reads=[f"dn{po}"], writes=[f"dn{po}"])
                        for jj in range(4):
                            P.op("act", lambda e, jj=jj: e.activation(out=hat[po][:, jj, :], in_=PA[po][:, jj * 65:jj * 65 + 64],
                                                                       func=AF.Copy, scale=dn[:, po, jj:jj + 1]),
                                 reads=[f"PA{po}", f"dn{po}"], writes=[f"hat{po}"])
                        pt = next_pt()
                        hv = hat[po][:].rearrange("p h d -> p (h d)")
                        for c in range(2):
                            P.op("pe", lambda e, c=c: e.transpose(out=PT[pt][:, c * 128:(c + 1) * 128],
                                                                  in_=hv[:, c * 128:(c + 1) * 128], identity=ident[:]),
                                 reads=[f"hat{po}", "ident"], writes=[f"PT{pt}"])
                        P.op("dve", lambda e: e.tensor_copy(
                            out=haT[:, 2 * g:2 * g + 2, i * 128:(i + 1) * 128],
                            in_=PT[pt][:, 0:256].rearrange("p (c t) -> p c t", c=2)),
                            reads=[f"PT{pt}"], writes=["haT"])

                DEPTH = ND - 1
                for n in range(min(DEPTH, len(items))):
                    stage_a(n)
                for n in range(len(items)):
                    if n + DEPTH < len(items):
                        stage_a(n + DEPTH)
                    stage_b(n)
                P.barrier()

        def phase_tail(l, b):
            with ExitStack() as ph:
                def psb(name, shape, dt=F32):
                    uid[0] += 1
                    return ph.enter_context(nc.sbuf_tensor(f"ph_{name}_{uid[0]}", list(shape), dt))
                yT = psb("yT", [128, 8, TB], BF16)
                sga = [psb(f"sga{i}", [128, 512]) for i in range(2)]
                sgb = [psb(f"sgb{i}", [128, 512]) for i in range(2)]
                it = 0
                for ncx in range(8):
                    n0 = ncx * 128
                    ws = load_w([wcols(w_m_d[l], n0, 128, 0), wcols(w_a_d[l], n0, 128, 128),
                                 wcols(w_in_d[l], OFF_GA + n0, 128, 256), wcols(w_in_d[l], OFF_GB + n0, 128, 384)])
                    for tg in range(2):
                        t0 = tg * 512
                        u = it % 2
                        it += 1
                        srcs = {"ga": (hT, 256, 128, HT_BLK), "gb": (hT, 384, 128, HT_BLK),
                                "m": (hmT, 0, 0, ["hmT"]), "a": (haT, 128, 0, ["haT"])}
                        bank = {}
                        for key in ("ga", "gb", "m", "a"):
                            src, wc, toff, rd = srcs[key]
                            pa = next_pa(6)
                            bank[key] = pa
                            for k in range(8):
                                P.op("pe", lambda e, k=k: e.matmul(
                                    PA[pa][:, 0:512], lhsT=wb[ws][:, k, wc:wc + 128],
                                    rhs=src[:, k, toff + t0:toff + t0 + 512], start=(k == 0), stop=(k == 7)),
                                    reads=[f"wb{ws}"] + rd, writes=[f"PA{pa}"])
                        P.op("act", lambda e: e.activation(out=sga[u][:], in_=PA[bank["ga"]][:, 0:512], func=AF.Sigmoid),
                             reads=[f"PA{bank['ga']}"], writes=[f"sga{u}"])
                        P.op("act", lambda e: e.activation(out=sgb[u][:], in_=PA[bank["gb"]][:, 0:512], func=AF.Sigmoid),
                             reads=[f"PA{bank['gb']}"], writes=[f"sgb{u}"])
                        P.op("dve", lambda e: e.tensor_tensor(out=sga[u][:], in0=PA[bank["m"]][:, 0:512], in1=sga[u][:], op=ALU.mult),
                             reads=[f"PA{bank['m']}", f"sga{u}"], writes=[f"sga{u}"])
                        P.op("dve", lambda e: e.tensor_tensor(out=sgb[u][:], in0=PA[bank["a"]][:, 0:512], in1=sgb[u][:], op=ALU.mult),
                             reads=[f"PA{bank['a']}", f"sgb{u}"], writes=[f"sgb{u}"])
                        P.op("dve", lambda e: e.tensor_tensor(out=yT[:, ncx, t0:t0 + 512], in0=sga[u][:], in1=sgb[u][:],
                                                              op=ALU.add),
                             reads=[f"sga{u}", f"sgb{u}"], writes=["yT"])
                for nh in range(2):
                    ws = load_w([wcols(w_o_d[l], nh * 512, 512, 0)])
                    for i in range(NTB):
                        gt_ = b * NTB + i
                        pa = next_pa(6)
                        for k in range(8):
                            P.op("pe", lambda e, k=k: e.matmul(PA[pa][:, 0:512], lhsT=yT[:, k, i * 128:(i + 1) * 128],
                                                               rhs=wb[ws][:, k, 0:512], start=(k == 0), stop=(k == 7)),
                                 reads=[f"wb{ws}", "yT"], writes=[f"PA{pa}"])
                        xs = x[:, gt_, nh * 512:(nh + 1) * 512]
                        P.op("dve", lambda e: e.tensor_tensor(out=xs, in0=PA[pa][:, 0:512], in1=xs, op=ALU.add),
                             reads=[f"PA{pa}", f"x{gt_}"], writes=[f"x{gt_}"])
                P.barrier()

        def phase_mlp(l, b):
            with ExitStack() as ph:
                def psb(name, shape, dt=F32):
                    uid[0] += 1
                    return ph.enter_context(nc.sbuf_tensor(f"ph_{name}_{uid[0]}", list(shape), dt))
                uT = [psb(f"uT{i}", [128, 4, TB], BF16) for i in range(2)]
                rt = [psb(f"rt{i}", [128, 512]) for i in range(2)]
                phase_norm(l, b, 2 * l + 1, with_halo=False)
                for fg in range(8):
                    wu = load_w([wcols(w_up_d[l], fg * 512, 512, 0)])
                    wd = load_w([(w_dn_d[l][fg * 512:(fg + 1) * 512, :].rearrange("(k p) n -> p k n", p=128),
                                  lambda w: w[:].rearrange("p k n -> p (k n)").rearrange("p (k n) -> p k n", k=4))])
                    wdv = wb[wd][:].rearrange("p k n -> p (k n)").rearrange("p (k n) -> p k n", k=4)
                    u = uT[fg % 2]
                    un = f"uT{fg % 2}"
                    for fc in range(4):
                        for tg in range(2):
                            r_ = rt[(fc * 2 + tg) % 2]
                            rn = f"rt{(fc * 2 + tg) % 2}"
                            def uev(pa, fc=fc, tg=tg, r_=r_, rn=rn, u=u, un=un):
                                P.op("act", lambda e: e.activation(out=r_[:], in_=PA[pa][:, 0:512], func=AF.Relu),
                                     reads=[f"PA{pa}"], writes=[rn])
                                P.op("act", lambda e: e.activation(out=u[:, fc, tg * 512:(tg + 1) * 512], in_=r_[:], func=AF.Square),
                                     reads=[rn], writes=[un])
                            proj_fm(wu, fc * 128, 128, 128 + tg * 512, 512, uev)
                    for i in range(NTB):
                        gt_ = b * NTB + i
                        for nh in range(2):
                            pa = next_pa()
                            for fc in range(4):
                                P.op("pe", lambda e, fc=fc, i=i, nh=nh, pa=pa: e.matmul(
                                    PA[pa][:, 0:512], lhsT=u[:, fc, i * 128:(i + 1) * 128],
                                    rhs=wdv[:, fc, nh * 512:(nh + 1) * 512], start=(fc == 0), stop=(fc == 3)),
                                    reads=[f"wb{wd}", un], writes=[f"PA{pa}"])
                            xs = x[:, gt_, nh * 512:(nh + 1) * 512]
                            P.op("dve", lambda e, xs=xs, pa=pa: e.tensor_tensor(out=xs, in0=PA[pa][:, 0:512], in1=xs, op=ALU.add),
                                 reads=[f"PA{pa}", f"x{gt_}"], writes=[f"x{gt_}"])
                P.barrier()

        def phase_exchange(l):
            if mode == "main":
                gsrc = gath_in
            else:
                gsrc = gath[l].ap()
                P.dma("pool", lambda e: e.dma_start(out=bounce[l].ap(), in_=Cst[:].rearrange("p h c -> p (h c)")),
                      f"bnc{l}", reads=["Cst"], writes=[f"bounce{l}"])
                P.cc(lambda e: e.collective_compute("AllGather", ALU.bypass, replica_groups=[list(range(NCORES))],
                                                    ins=[bounce[l].ap().opt()], outs=[gath[l].ap().opt()]),
                     f"ag{l}", reads=[f"bounce{l}"], writes=[f"gath{l}"])
            P.op("dve", lambda e: e.memset(Cst[:], 0.0), reads=["Cst"], writes=["Cst"])
            stg = scr[:, 0:4 * CW]
            for r in range(NCORES):
                P.dma("sp", lambda e, r=r: e.dma_start(out=stg, in_=gsrc[r * 128:(r + 1) * 128, :]),
                      "scr", reads=[f"gath{l}", "scrA", "scrB"], writes=["scr", "scrA", "scrB"])
                s_r = flags[:, r:r + 1]
                for h in range(4):
                    dcol = scr[:, h * CW + 257:h * CW + 258]
                    dd = ss[:, 5:6]
                    P.op("dve", lambda e, dcol=dcol, s_r=s_r, dd=dd: e.tensor_scalar(
                        out=dd, in0=dcol, scalar1=-1.0, scalar2=s_r, op0=ALU.add, op1=ALU.mult),
                        reads=["scr", "flags"], writes=["ss5"])
                    P.op("dve", lambda e, dd=dd: e.tensor_scalar(out=dd, in0=dd, scalar1=1.0, scalar2=None, op0=ALU.add),
                         reads=["ss5"], writes=["ss5"])
                    P.op("dve", lambda e, h=h, dd=dd: e.tensor_scalar(out=Cst[:, h, 0:257], in0=Cst[:, h, 0:257],
                                                                      scalar1=dd, scalar2=None, op0=ALU.mult),
                         reads=["ss5", "Cst"], writes=["Cst"])
                    P.op("dve", lambda e, h=h, s_r=s_r: e.scalar_tensor_tensor(
                        out=Cst[:, h, 0:257], in0=scr[:, h * CW:h * CW + 257], scalar=s_r, in1=Cst[:, h, 0:257],
                        op0=ALU.mult, op1=ALU.add), reads=["scr", "flags", "Cst"], writes=["Cst"])
            P.barrier()

        def phase_exchange_halo():
            P.dma("pool", lambda e: e.dma_start(out=bounce_h.ap(), in_=x[:, NTL - 1, :]),
                  "bnch", reads=[f"x{NTL - 1}"], writes=["bounce_h"])
            P.cc(lambda e: e.collective_compute("AllGather", ALU.bypass, replica_groups=[list(range(NCORES))],
                                                ins=[bounce_h.ap().opt()], outs=[gath_h.ap().opt()]),
                 "agh", reads=["bounce_h"], writes=["gath_h"])
            P.op("dve", lambda e: e.memset(xhalo[:], 0.0), reads=["xhalo"], writes=["xhalo"])
            stg = scr[:, 0:D]
            for r in range(NCORES):
                P.dma("sp", lambda e, r=r: e.dma_start(out=stg, in_=gath_h.ap()[r * 128:(r + 1) * 128, :]),
                      "scr", reads=["gath_h", "scrA", "scrB"], writes=["scr", "scrA", "scrB"])
                p_r = flags[:, 8 + r:9 + r]
                P.op("dve", lambda e, p_r=p_r: e.scalar_tensor_tensor(
                    out=xhalo[:], in0=stg, scalar=p_r, in1=xhalo[:], op0=ALU.mult, op1=ALU.add),
                    reads=["scr", "flags", "xhalo"], writes=["xhalo"])
            P.barrier()

        steps = []
        for l in LAYERS:
            def prolog(l=l):
                cload(gm[:], gm_d[:, l, :], "gm")
                P.op("act", lambda e, l=l: e.activation(out=esink[:], in_=sinks[:, l, :], func=AF.Exp),
                     reads=["sinks"], writes=["esink"])
                if l > l0 and not no_cc:
                    phase_exchange_halo()
                P.op("dve", lambda e: e.memset(Cst[:], 0.0), reads=["Cst"], writes=["Cst"])
                P.op("dve", lambda e: e.memset(Cst[:, :, 257:258], 1.0), reads=["Cst"], writes=["Cst"])
            steps.append(prolog)
            if mode != "main":
                for b in range(NBLK):
                    steps.append(lambda l=l, b=b: phase_norm(l, b, 2 * l))
                    steps.append(lambda l=l, b=b: phase_gates(l, b))
                    for h in range(4):
                        steps.append(lambda l=l, b=b, h=h: phase_mlstm_head(l, b, h, 1))
            if mode == "sum":
                continue
            def exch(l=l):
                if no_cc:
                    P.op("dve", lambda e: e.memset(Cst[:], 0.0), reads=["Cst"], writes=["Cst"])
                else:
                    phase_exchange(l)
            steps.append(exch)
            for b in range(NBLK):
                steps.append(lambda l=l, b=b: phase_norm(l, b, 2 * l))
                steps.append(lambda l=l, b=b: phase_gates(l, b))
                for h in range(4):
                    steps.append(lambda l=l, b=b, h=h: phase_mlstm_head(l, b, h, 2))
                for g in range(4):
                    steps.append(lambda l=l, b=b, g=g: phase_attn(l, b, g))
                steps.append(lambda l=l, b=b: phase_tail(l, b))
                steps.append(lambda l=l, b=b: phase_mlp(l, b))
        for si, stp in enumerate(steps):
            if nsteps is not None and si >= nsteps:
                break
            stp()
        otoks = []
        if mode == "sum":
            tcs = P.dma("sp", lambda e: e.dma_start(out=cst_out, in_=Cst[:].rearrange("p h c -> p (h c)")), "out", reads=["Cst"])
            P.final_wait("sp", [tcs])
            P.emit(st)
            return nc
        if dbg:
            otoks.append(P.dma("sp", lambda e: e.dma_start(out=d_hT, in_=hT[:].rearrange("p c t -> p (c t)")), "out", reads=HT_ALL))
            otoks.append(P.dma("sp", lambda e: e.dma_start(out=d_hmT, in_=hmT[:].rearrange("p c t -> p (c t)")), "out", reads=["hmT"]))
            otoks.append(P.dma("sp", lambda e: e.dma_start(out=d_haT, in_=haT[:].rearrange("p c t -> p (c t)")), "out", reads=["haT"]))
            otoks.append(P.dma("sp", lambda e: e.dma_start(out=d_misc[:, 0:4 * CW], in_=Cst[:].rearrange("p h c -> p (h c)")), "out", reads=["Cst"]))
            for qi, (buf, nm) in enumerate([(gsum[:].rearrange("p i c -> p (i c)"), "gsum"), (spt[:], "spt"), (a_t[:], "a_t"), (ea[:], "ea"), (ksc[:], "ksc"), (ebL[:], "ebL")]):
                w_ = 64 if nm == "gsum" else 32
                o0 = 4 * CW + (0 if qi == 0 else 64 + (qi - 1) * 32)
                otoks.append(P.dma("sp", lambda e, buf=buf, o0=o0, w_=w_: e.dma_start(out=d_misc[:, o0:o0 + w_], in_=buf), "out", reads=[nm]))
        for i in range(NTL):
            xi = x[:, i, :]
            P.op("act", lambda e, xi=xi: e.activation(out=junk[:], in_=xi, func=AF.Square, accum_out=ss[:, 0:1]),
                 reads=[f"x{i}"], writes=["junk", "ss"])
            rstd_from_ss(0, 1.0 / D)
            ob = scr[:, (i % 2) * D:(i % 2 + 1) * D]
            on = f"ob{i % 2}"
            if not final:
                P.op("dve", lambda e, xi=xi, ob=ob: e.tensor_copy(out=ob, in_=xi), reads=[f"x{i}", "scr", "scrA", "scrB"], writes=[on])
            else:
                P.op("dve", lambda e, xi=xi, ob=ob: e.scalar_tensor_tensor(out=ob, in0=xi, scalar=ss[:, 0:1], in1=gfin[:],
                                                                           op0=ALU.mult, op1=ALU.mult),
                     reads=[f"x{i}", "ss", "gfin", "scr", "scrA", "scrB"], writes=[on])
            otoks.append(P.dma("sp", lambda e, i=i, ob=ob: e.dma_start(out=out_d[i * 128:(i + 1) * 128, :], in_=ob),
                               "out", reads=[on]))
        P.final_wait("sp", [otoks[-1]])
        P.emit(st)
    return nc


def _t5_bucket(n):
    n_buckets, max_distance = 32, 128
    max_exact = n_buckets // 2
    n = np.maximum(n, 0)
    large = max_exact + (np.log(np.maximum(n, 1) / max_exact) / np.log(max_distance / max_exact)
                         * (n_buckets - max_exact)).astype(np.int32)
    large = np.minimum(large, n_buckets - 1)
    return np.where(n < max_exact, n, large).astype(np.int32)


def _host_layout(inputs, layers=(0, 1), xsrc=None):
    f = lambda a: np.ascontiguousarray(np.asarray(a), dtype=np.float32)
    x = f(inputs["x"]) if xsrc is None else xsrc
    rep = lambda v: np.ascontiguousarray(np.broadcast_to(v, (128,) + v.shape))
    common = {}
    for l in layers:
        for nm in ("w_in", "w_branch_m", "w_branch_a", "w_out", "w_up", "w_down"):
            common[f"{nm}{l}"] = np.ascontiguousarray(f(inputs[nm])[l])
    gall = np.stack([f(inputs["norm_mix_g"])[0], f(inputs["norm_mlp_g"])[0],
                     f(inputs["norm_mix_g"])[1], f(inputs["norm_mlp_g"])[1]], 0)
    common["gcols"] = np.ascontiguousarray(gall.reshape(4, 8, 128).transpose(2, 0, 1))
    common["gfin"] = rep(f(inputs["final_norm_g"]))
    cw = f(inputs["conv_w"])
    common["convw"] = np.ascontiguousarray(cw.reshape(2, 4, 8, 128).transpose(3, 0, 2, 1))
    common["convb"] = np.ascontiguousarray(f(inputs["conv_b"]).reshape(2, 8, 128).transpose(2, 0, 1))
    gb = np.concatenate([f(inputs["b_igate"]), f(inputs["b_fgate"])], axis=1)
    common["gateb"] = rep(np.ascontiguousarray(np.tile(gb[:, None, :], (1, 8, 1)).reshape(2, 64)))
    common["gm"] = rep(f(inputs["mlstm_norm_g"]))
    common["sinks"] = rep(f(inputs["attn_sinks"]))
    kk = np.arange(128)[:, None]
    qq = np.arange(128)[None, :]
    rb = f(inputs["rel_bias"])
    biasT = np.zeros((128, 16, 2, 128), np.float32)
    maskT = np.zeros((128, 2, 128), np.float32)
    for blk in range(2):
        dist = qq - kk + (128 if blk == 0 else 0)
        valid = (dist >= 0) & (dist < 128)
        bt = rb[_t5_bucket(dist)]
        biasT[:, :, blk, :] = np.where(valid[:, None, :], bt.transpose(0, 2, 1), 0.0)
        maskT[:, blk, :] = np.where(valid, 0.0, NEG)
    common["biasT"] = biasT
    common["maskT"] = maskT
    in_maps = []
    for r in range(NCORES):
        sq, j = r // 4, r % 4
        m = dict(common)
        m["x"] = np.ascontiguousarray(x[sq, j * TL:(j + 1) * TL, :])
        m["xh"] = (np.ascontiguousarray(x[sq, j * TL - 128:j * TL, :]) if j > 0 else np.zeros((128, D), np.float32))
        fl = np.zeros((128, 24), np.float32)
        for rp in range(NCORES):
            if rp // 4 == sq and rp < r:
                fl[:, rp] = 1.0
        if j > 0:
            fl[:, 8 + r - 1] = 1.0
            fl[:, 16] = 1.0
        m["flags"] = fl
        in_maps.append(m)
    return in_maps


def run(inputs, depth=2, dbg=False, nsteps=None, no_cc=False, ncores=NCORES, l0=0, final=None, xsrc=None,
        mode="fused", gath_in=None):
    in_maps = _host_layout(inputs, layers=list(range(l0, l0 + depth)), xsrc=xsrc)[:ncores]
    if gath_in is not None:
        for m in in_maps:
            m["gath_in"] = gath_in
    nc = build_program(depth=depth, dbg=dbg, nsteps=nsteps, no_cc=no_cc, l0=l0, final=final, mode=mode)
    res = run_bass_kernel_spmd(nc, in_maps, core_ids=list(range(ncores)))
    if mode == "sum":
        return np.concatenate([res.results[r]["cst_out"] for r in range(ncores)], axis=0)
    global LAST_RES
    LAST_RES = res
    out = np.zeros((2, SEQ, D), np.float32)
    for r in range(ncores):
        out[r // 4, (r % 4) * TL:(r % 4 + 1) * TL, :] = res.results[r]["out"]
    return out


def kernel(**inputs):
    xcur = None
    for l in range(2):
        g = run(inputs, depth=1, l0=l, mode="sum", xsrc=xcur)
        xcur = run(inputs, depth=1, l0=l, mode="main", final=(l == 1), xsrc=xcur, gath_in=g)
    return xcur
```

```python
import types
import numpy as np
from contextlib import ExitStack
import concourse.bass as bass
import concourse.mybir as mybir
from concourse.bass_utils import run_bass_kernel_spmd

F32 = mybir.dt.float32
BF16 = mybir.dt.bfloat16
AF = mybir.ActivationFunctionType
ALU = mybir.AluOpType

D = 1024
SEQ = 8192
NCORES = 8
TL = 2048
NTL = 16
NTB = 8
TB = 1024
NBLK = 2
NIN = 6664
DFF = 4096
EPS = 1e-6
OFF_MQ, OFF_MK, OFF_MV, OFF_MO, OFF_MI, OFF_AQ, OFF_AK, OFF_AV, OFF_GA, OFF_GB = (
    0, 512, 1024, 2048, 3072, 3080, 4104, 4360, 4616, 5640)
QSCALE = 128 ** -0.5
CW = 260
XW = 4 * CW + D
NEG = -30000.0

ENGS = ("pe", "act", "dve", "pool", "sp")
CUT = [0]


def _snap(fn):
    if fn is None or fn.__closure__ is None:
        return fn
    cells = []
    for c in fn.__closure__:
        try:
            cells.append(types.CellType(c.cell_contents))
        except ValueError:
            cells.append(c)
    g = types.FunctionType(fn.__code__, fn.__globals__, fn.__name__, fn.__defaults__, tuple(cells))
    g.__kwdefaults__ = fn.__kwdefaults__
    return g


class Prog:
    def __init__(self, nc):
        self.nc = nc
        self.ops = {e: [] for e in ENGS}
        self.count = {e: 0 for e in ENGS}
        self.waited = {e: {} for e in ENGS}
        self.last_write = {}
        self.readers = {}
        self.dma_count = {}
        self.cc_keys = []
        self.sem_handles = {}

    def _deps(self, reads, writes):
        deps = []
        for r in reads:
            if r in self.last_write:
                deps.append(self.last_write[r])
        for w in writes:
            if w in self.last_write:
                deps.append(self.last_write[w])
            deps.extend(self.readers.get(w, ()))
        return deps

    def _commit(self, token, reads, writes):
        for r in reads:
            self.readers.setdefault(r, []).append(token)
        for w in writes:
            self.last_write[w] = token
            self.readers[w] = []

    def _waits(self, eng, deps):
        need = {}
        for (k, v) in deps:
            if k == "pe" and eng == "pe":
                continue
            if self.waited[eng].get(k, 0) >= v:
                continue
            need[k] = max(need.get(k, 0), v)
        for k, v in need.items():
            self.waited[eng][k] = v
        return list(need.items())

    def op(self, eng, fn, reads=(), writes=()):
        psr = [r for r in reads if r.startswith("PA") or r.startswith("PT")]
        if psr:
            reads = [r for r in reads if r not in psr]
            writes = list(writes) + [r for r in psr if r not in writes]
        waits = self._waits(eng, self._deps(reads, writes))
        self.count[eng] += 1
        token = (eng, self.count[eng])
        self.ops[eng].append(("op", _snap(fn), waits, None))
        self._commit(token, reads, writes)
        return token

    def dma(self, eng, fn, group, reads=(), writes=()):
        waits = self._waits(eng, self._deps(reads, writes))
        key = "dma:" + group
        self.dma_count[key] = self.dma_count.get(key, 0) + 1
        token = (key, 16 * self.dma_count[key])
        self.ops[eng].append(("dma", _snap(fn), waits, key))
        self._commit(token, reads, writes)
        return token

    def cc(self, fn, name, reads=(), writes=()):
        alld = [(k, 16 * c) for k, c in self.dma_count.items()]
        alle = [(e, self.count[e]) for e in ("pe", "act", "dve", "pool") if self.count[e] > 0]
        waits = self._waits("pool", self._deps(reads, writes) + alld + alle)
        key = "cc:all"
        if key not in self.cc_keys:
            self.cc_keys.append(key)
        self.cc_n = getattr(self, "cc_n", 0) + 1
        token = (key, self.cc_n)
        self.ops["pool"].append(("cc", _snap(fn), waits, key))
        for e in ENGS:
            self.ops[e].append(("wait", None, [token], None))
            self.waited[e][key] = self.cc_n
        self._commit(token, reads, writes)
        return token

    def barrier(self):
        comp = ("pe", "act", "dve")
        toks = [(e, self.count[e]) for e in comp if self.count[e] > 0]
        for e in comp:
            need = []
            for (k, v) in toks:
                if self.waited[e].get(k, 0) >= v:
                    continue
                self.waited[e][k] = v
                need.append((k, v))
            if need:
                self.ops[e].append(("wait", None, need, None))

    def final_wait(self, eng, tokens):
        waits = self._waits(eng, tokens)
        self.ops[eng].append(("wait", None, waits, None))

    def emit(self, stack):
        nc = self.nc
        keys = list(ENGS) + sorted(self.dma_count.keys()) + self.cc_keys
        for k in keys:
            self.sem_handles[k] = stack.enter_context(
                nc.semaphore("s_" + k.replace(":", "_").replace(".", "_")))
        block = stack.enter_context(nc.Block())
        sh = self.sem_handles

        def make(engname):
            def body(e):
                for (kind, fn, waits, key) in self.ops[engname]:
                    for (k, v) in waits:
                        e.wait_ge(sh[k], v)
                    if kind == "op":
                        fn(e).then_inc(sh[engname], 1)
                    elif kind == "dma":
                        fn(e).then_inc(sh[key], 16)
                    elif kind == "cc":
                        fn(e).then_inc(sh[key])
            return body

        block.tensor(make("pe"))
        block.scalar(make("act"))
        block.vector(make("dve"))
        block.gpsimd(make("pool"))
        block.sync(make("sp"))


def build_program(depth=2, dbg=False, nsteps=None, no_cc=False, l0=0, final=None, mode="fused"):
    final = (not dbg) if final is None else final
    LAYERS = list(range(l0, l0 + depth))
    nc = bass.Bass("TRN2", target_bir_lowering=False)

    def din(name, shape, dt=F32):
        return nc.dram_tensor(name, list(shape), dt, kind="ExternalInput").ap()

    x_d = din("x", [TL, D])
    xh_d = din("xh", [128, D])
    flags_d = din("flags", [128, 24])
    w_in_d = {l: din(f"w_in{l}", [D, NIN]) for l in LAYERS}
    w_m_d = {l: din(f"w_branch_m{l}", [D, D]) for l in LAYERS}
    w_a_d = {l: din(f"w_branch_a{l}", [D, D]) for l in LAYERS}
    w_o_d = {l: din(f"w_out{l}", [D, D]) for l in LAYERS}
    w_up_d = {l: din(f"w_up{l}", [D, DFF]) for l in LAYERS}
    w_dn_d = {l: din(f"w_down{l}", [DFF, D]) for l in LAYERS}
    gcols_d = din("gcols", [128, 4, 8])
    gfin_d = din("gfin", [128, D])
    convw_d = din("convw", [128, 2, 8, 4])
    convb_d = din("convb", [128, 2, 8])
    gateb_d = din("gateb", [128, 2, 64])
    gm_d = din("gm", [128, 2, D])
    sinks_d = din("sinks", [128, 2, 16])
    biasT_d = din("biasT", [128, 16, 2, 128])
    maskT_d = din("maskT", [128, 2, 128])
    if mode == "sum":
        cst_out = nc.dram_tensor("cst_out", [128, 4 * CW], F32, kind="ExternalOutput").ap()
    else:
        out_d = nc.dram_tensor("out", [TL, D], F32, kind="ExternalOutput").ap()
    if mode == "main":
        gath_in = din("gath_in", [NCORES * 128, 4 * CW])
    if dbg:
        d_hT = nc.dram_tensor("d_hT", [128, 8 * 9 * 128], BF16, kind="ExternalOutput").ap()
        d_hmT = nc.dram_tensor("d_hmT", [128, 8 * TB], BF16, kind="ExternalOutput").ap()
        d_haT = nc.dram_tensor("d_haT", [128, 8 * TB], BF16, kind="ExternalOutput").ap()
        d_misc = nc.dram_tensor("d_misc", [128, 4 * CW + 7 * 32], F32, kind="ExternalOutput").ap()
    bounce = [nc.dram_tensor(f"bounce{l}", [128, 4 * CW], F32) for l in range(2)]
    gath = [nc.dram_tensor(f"gath{l}", [NCORES * 128, 4 * CW], F32) for l in range(2)]
    bounce_h = nc.dram_tensor("bounce_h", [128, D], F32)
    gath_h = nc.dram_tensor("gath_h", [NCORES * 128, D], F32)

    st = ExitStack()
    with st:
        uid = [0]

        def sb(name, shape, dt=F32):
            uid[0] += 1
            return st.enter_context(nc.sbuf_tensor(f"sb_{name}_{uid[0]}", list(shape), dt))

        def ps(name, shape, dt=F32):
            uid[0] += 1
            return st.enter_context(nc.psum_tensor(f"ps_{name}_{uid[0]}", list(shape), dt))

        P = Prog(nc)
        x = sb("x", [128, NTL, D])
        xhalo = sb("xhalo", [128, D])
        hT = sb("hT", [128, 8, 9 * 128], BF16)
        hTh = sb("hTh", [128, 8, 128], BF16)
        hmT = sb("hmT", [128, 8, TB], BF16)
        haT = sb("haT", [128, 8, TB], BF16)
        NWB = 3
        wb = [sb(f"wb{i}", [128, 8, 512], BF16) for i in range(NWB)]
        ident = sb("ident", [128, 128], BF16)
        onesf = sb("onesf", [128, 128])
        trif = sb("trif", [128, 128])
        maskS = sb("maskS", [128, 128])
        biasm = sb("biasm", [128, 16, 2, 128], BF16)
        gcols = sb("gcols", [128, 4, 8])
        gfin = sb("gfin", [128, D])
        convw = sb("convw", [128, 2, 8, 4])
        convb = sb("convb", [128, 2, 8])
        gateb = sb("gateb", [128, 2, 64])
        gm = sb("gm", [128, D])
        sinks = sb("sinks", [128, 2, 16])
        esink = sb("esink", [128, 16])
        flags = sb("flags", [128, 24])
        Cst = sb("Cst", [128, 4, CW])
        gsum = sb("gsum", [128, 8, 8])
        spt = sb("spt", [128, 32])
        a_t = sb("a_t", [128, 32])
        ea = sb("ea", [128, 32])
        dtmp = sb("dtmp", [128, 32])
        ksc = sb("ksc", [128, 32])
        ebL = sb("ebL", [128, 32])
        ss = sb("ss", [128, 8])
        junk = sb("junk", [128, D], BF16)
        xn2 = [sb(f"xn{i}", [128, D], BF16) for i in range(2)]
        scr = sb("scr", [128, 2304])
        PA = [ps(f"PA{i}", [128, 512]) for i in range(6)]
        PT = [ps(f"PT{i}", [128, 1024], BF16) for i in range(2)]
        rot = {"pa": 0, "pt": 0, "wb": 0, "nrm": 0}

        def next_pa(n=4):
            i = rot["pa"] % n
            rot["pa"] += 1
            return i

        def next_pt():
            i = rot["pt"] % 2
            rot["pt"] += 1
            return i

        def load_w(parts):
            slot = rot["wb"] % NWB
            rot["wb"] += 1
            for (src, dst_fn) in parts:
                P.dma("pool", (lambda e, s=src, d=dst_fn(wb[slot]): e.dma_start(out=d, in_=s)),
                      f"wb{slot}", writes=[f"wb{slot}"])
            return slot

        def wcols(src2d, c0, n, dst0):
            s = src2d[:, c0:c0 + n].rearrange("(k p) n -> p k n", p=128)
            return (s, lambda w, a=dst0, b=n: w[:, :, a:a + b])

        def cload(dst, src, name):
            P.dma("sp", lambda e: e.dma_start(out=dst, in_=src), name, writes=[name])

        cload(gcols[:], gcols_d, "gcols")
        cload(gfin[:], gfin_d, "gfin")
        cload(convw[:], convw_d, "convw")
        cload(convb[:], convb_d, "convb")
        cload(gateb[:], gateb_d, "gateb")
        cload(sinks[:], sinks_d, "sinks")
        cload(flags[:], flags_d, "flags")
        cload(xhalo[:], xh_d, "xhalo")
        cload(maskS[:], maskT_d[:, 0, :], "maskS")
        cload(trif[:], maskT_d[:, 1, :], "trif")
        for hh in range(2):
            P.dma("sp", lambda e, hh=hh: e.dma_start(
                out=scr[:, 0:2048].rearrange("p (h b q) -> p h b q", h=8, b=2),
                in_=biasT_d[:, hh * 8:(hh + 1) * 8, :, :]), "scr", writes=["scr"])
            for h8 in range(8):
                for blk, mt in ((0, maskS), (1, trif)):
                    P.op("dve", lambda e, hh=hh, h8=h8, blk=blk, mt=mt: e.tensor_tensor(
                        out=biasm[:, hh * 8 + h8, blk, :],
                        in0=scr[:, (h8 * 2 + blk) * 128:(h8 * 2 + blk + 1) * 128],
                        in1=mt[:], op=ALU.add),
                        reads=["scr", "maskS", "trif"], writes=["biasm"])
        P.op("pool", lambda e: e.memset(onesf[:], 1.0), writes=["onesf"])
        P.op("pool", lambda e: e.affine_select(out=ident[:], in_=onesf[:], pattern=[[1, 128]], base=0,
                                                 channel_multiplier=-1, compare_op=ALU.is_equal, fill=0.0),
             reads=["onesf"], writes=["ident"])
        P.op("pool", lambda e: e.affine_select(out=trif[:], in_=onesf[:], pattern=[[1, 128]], base=0,
                                                 channel_multiplier=-1, compare_op=ALU.is_ge, fill=0.0),
             reads=["onesf", "biasm"], writes=["trif"])
        P.op("dve", lambda e: e.tensor_scalar(out=maskS[:], in0=trif[:], scalar1=QSCALE, scalar2=None,
                                               op0=ALU.mult),
             reads=["trif", "biasm"], writes=["maskS"])
        for i in range(NTL):
            P.dma("sp", lambda e, i=i: e.dma_start(out=x[:, i, :], in_=x_d[i * 128:(i + 1) * 128, :]),
                  "xin", writes=[f"x{i}"])
        tok_x = P.last_write[f"x{NTL - 1}"]
        for i in range(NTL):
            P.last_write[f"x{i}"] = tok_x

        def rstd_from_ss(col, inv_n, nm="ss"):
            c = ss[:, col:col + 1]
            P.op("act", lambda e: e.activation(out=c, in_=c, func=AF.Ln, scale=inv_n, bias=EPS), reads=[nm], writes=[nm])
            P.op("act", lambda e: e.activation(out=c, in_=c, func=AF.Exp, scale=-0.5), reads=[nm], writes=[nm])

        def norm_to_hT(xsrc, xres, n_idx, slot):
            par = rot["nrm"] % 2
            rot["nrm"] += 1
            xn = xn2[par]
            xnn = f"xn{par}"
            scol = 0 if par == 0 else 6
            snm = f"ssn{par}"
            P.op("act", lambda e: e.activation(out=junk[:], in_=xsrc, func=AF.Square, accum_out=ss[:, scol:scol + 1]),
                 reads=[xres], writes=["junk", snm])
            rstd_from_ss(scol, 1.0 / D, snm)
            P.op("dve", lambda e: e.tensor_scalar(out=xn[:], in0=xsrc, scalar1=ss[:, scol:scol + 1], scalar2=None,
                                                   op0=ALU.mult), reads=[xres, snm], writes=[xnn])
            pt = next_pt()
            for c in range(8):
                P.op("pe", lambda e, c=c: e.transpose(out=PT[pt][:, c * 128:(c + 1) * 128],
                                                       in_=xn[:, c * 128:(c + 1) * 128], identity=ident[:]),
                     reads=[xnn, "ident"], writes=[f"PT{pt}"])
            for c in range(8):
                eng = "act" if c % 2 == 0 else "dve"
                o = hT[:, c, slot * 128:(slot + 1) * 128]
                i_ = PT[pt][:, c * 128:(c + 1) * 128]
                g = gcols[:, n_idx, c:c + 1]
                if eng == "act":
                    P.op("act", lambda e, o=o, i_=i_, g=g: e.activation(out=o, in_=i_, func=AF.Copy, scale=g),
                         reads=[f"PT{pt}", "gcols"], writes=[f"hT{slot}"])
                else:
                    P.op("dve", lambda e, o=o, i_=i_, g=g: e.tensor_scalar(out=o, in0=i_, scalar1=g,
                                                                           scalar2=None, op0=ALU.mult),
                         reads=[f"PT{pt}", "gcols"], writes=[f"hT{slot}"])

        HT_ALL = [f"hT{s}" for s in range(9)]
        HT_BLK = [f"hT{s}" for s in range(1, 9)]

        def proj_fm(wslot, wc0, ncol, t0, tn, evac):
            pa = next_pa()
            for k in range(8):
                P.op("pe", lambda e, k=k: e.matmul(PA[pa][0:ncol, 0:tn], lhsT=wb[wslot][:, k, wc0:wc0 + ncol],
                                                    rhs=hT[:, k, t0:t0 + tn], start=(k == 0), stop=(k == 7)),
                     reads=[f"wb{wslot}"] + HT_ALL, writes=[f"PA{pa}"])
            evac(pa)

        def proj_tm(wslot, wc0, ncol, slot, evac):
            pa = next_pa()
            for k in range(8):
                P.op("pe", lambda e, k=k: e.matmul(PA[pa][:, 0:ncol], lhsT=hT[:, k, slot * 128:(slot + 1) * 128],
                                                    rhs=wb[wslot][:, k, wc0:wc0 + ncol], start=(k == 0), stop=(k == 7)),
                     reads=[f"wb{wslot}", f"hT{slot}"], writes=[f"PA{pa}"])
            evac(pa)

        def phase_norm(l, b, n_idx, with_halo=True):
            if with_halo:
                if b == 0:
                    norm_to_hT(xhalo[:], "xhalo", n_idx, 0)
                else:
                    P.op("dve", lambda e: e.tensor_copy(out=hT[:, :, 0:128], in_=hTh[:]),
                         reads=["hTh"], writes=["hT0"])
            for i in range(NTB):
                gt_ = b * NTB + i
                norm_to_hT(x[:, gt_, :], f"x{gt_}", n_idx, i + 1)
            if with_halo and b == 0:
                P.op("dve", lambda e: e.tensor_copy(out=hTh[:], in_=hT[:, :, 8 * 128:9 * 128]),
                     reads=["hT8"], writes=["hTh"])

        def phase_gates(l, b):
            wslot = load_w([wcols(w_in_d[l], OFF_MI, 8, 0)])
            G = PA[5]
            for i in range(NTB):
                for k in range(8):
                    P.op("pe", lambda e, i=i, k=k: e.matmul(G[:, i * 8:(i + 1) * 8],
                                                            lhsT=hT[:, k, (i + 1) * 128:(i + 2) * 128],
                                                            rhs=wb[wslot][:, k, 0:8], start=(k == 0), stop=(k == 7)),
                         reads=[f"wb{wslot}", f"hT{i + 1}"], writes=["PA5"])
            if CUT[0] == 1:
                return
            P.op("dve", lambda e: e.tensor_tensor(out=gsum[:].rearrange("p i c -> p (i c)"), in0=G[:, 0:64],
                                                   in1=gateb[:, l, :], op=ALU.add),
                 reads=["PA5", "gateb"], writes=["gsum"])
            if CUT[0] == 2:
                return
            spv = spt[:].rearrange("p (i h) -> p i h", h=4)
            P.op("act", lambda e: e.activation(out=spv, in_=gsum[:, :, 4:8], func=AF.Exp, scale=-1.0),
                 reads=["gsum"], writes=["spt"])
            P.op("act", lambda e: e.activation(out=spt[:], in_=spt[:], func=AF.Ln, bias=1.0),
                 reads=["spt"], writes=["spt"])
            if CUT[0] == 3:
                return
            P.op("pe", lambda e: e.matmul(G[:, 64:96], lhsT=trif[:], rhs=spt[:], start=True, stop=True),
                 reads=["trif", "spt", "gsum"], writes=["PA5"])
            P.op("pe", lambda e: e.matmul(G[:, 96:128], lhsT=onesf[:], rhs=spt[:], start=True, stop=True),
                 reads=["onesf", "spt"], writes=["PA5"])
            if CUT[0] == 4:
                return
            P.op("dve", lambda e: e.tensor_tensor(out=a_t[:].rearrange("p (i h) -> p i h", h=4), in0=gsum[:, :, 0:4],
                                                   in1=G[:, 64:96].rearrange("p (i h) -> p i h", h=4), op=ALU.add),
                 reads=["PA5", "gsum"], writes=["a_t"])
            P.op("act", lambda e: e.activation(out=ea[:], in_=a_t[:], func=AF.Exp), reads=["a_t"], writes=["ea"])
            if CUT[0] == 5:
                return
            P.op("dve", lambda e: e.tensor_tensor(out=dtmp[:], in0=a_t[:], in1=G[:, 96:128], op=ALU.subtract),
                 reads=["PA5", "a_t"], writes=["dtmp"])
            P.op("act", lambda e: e.activation(out=ksc[:], in_=dtmp[:], func=AF.Exp), reads=["dtmp"], writes=["ksc"])
            if CUT[0] == 6:
                return
            P.op("act", lambda e: e.activation(out=ebL[:], in_=G[:, 96:128], func=AF.Exp, scale=-1.0),
                 reads=["PA5"], writes=["ebL"])

        def conv_silu(pa_evac_src, l, chunk, dst):
            acc = scr[:, 1152:1152 + TB]
            pre = scr
            w = lambda j: convw[:, l, chunk, j:j + 1]
            P.op("dve", lambda e: e.tensor_scalar(out=acc, in0=pre[:, 128:128 + TB], scalar1=w(3),
                                                   scalar2=convb[:, l, chunk:chunk + 1], op0=ALU.mult, op1=ALU.add),
                 reads=["scrA", "convw", "convb"], writes=["scrB"])
            for j in range(3):
                P.op("dve", lambda e, j=j: e.scalar_tensor_tensor(out=acc, in0=pre[:, 125 + j:125 + j + TB],
                                                                  scalar=w(j), in1=acc, op0=ALU.mult, op1=ALU.add),
                     reads=["scrA", "scrB"], writes=["scrB"])
            P.op("act", lambda e: e.activation(out=dst, in_=acc, func=AF.Silu), reads=["scrB"], writes=["qk"])

        def phase_mlstm_head(l, b, h, sweep):
            with ExitStack() as ph:
                def psb(name, shape, dt=F32):
                    uid[0] += 1
                    return ph.enter_context(nc.sbuf_tensor(f"ph_{name}_{uid[0]}", list(shape), dt))
                kT = psb("kT", [128, TB], BF16)
                ktok = psb("ktok", [128, NTB, 128], BF16)
                vaug = psb("vaug", [128, NTB, 257], BF16)
                if sweep == 2:
                    qT = psb("qT", [128, TB], BF16)
                    osig = psb("osig", [128, NTB, 256], BF16)
                    Lt = [psb(f"Lt{i}", [128, 128]) for i in range(2)]
                    Et = [psb(f"Et{i}", [128, 128]) for i in range(2)]
                    EM = [psb(f"EM{i}", [128, 128]) for i in range(2)]
                    qp = [psb(f"qp{i}", [128, 128], BF16) for i in range(2)]
                    Sb = [psb(f"Sb{i}", [128, 128], BF16) for i in range(2)]
                    hmt = [psb(f"hmt{i}", [128, 256], BF16) for i in range(2)]
                    t1 = [psb(f"t1{i}", [128, 256]) for i in range(2)]
                    Cb2 = [psb(f"Cb2{i}", [128, 257], BF16) for i in range(2)]
                    nrm = psb("nrm", [128, 2, 4])
                if sweep == 2:
                    ws = load_w([wcols(w_in_d[l], OFF_MQ + h * 128, 128, 0),
                                 wcols(w_in_d[l], OFF_MK + h * 128, 128, 128),
                                 wcols(w_in_d[l], OFF_MV + h * 256, 256, 256)])
                    ws2 = load_w([wcols(w_in_d[l], OFF_MO + h * 256, 256, 0)])
                    kc0 = 128
                else:
                    ws = load_w([wcols(w_in_d[l], OFF_MK + h * 128, 128, 128),
                                 wcols(w_in_d[l], OFF_MV + h * 256, 256, 256)])
                    kc0 = 128
                P.op("dve", lambda e: e.memset(vaug[:, :, 256:257], 1.0), writes=["vaug"])

                def pre_evac(tg):
                    def f(pa):
                        P.op("act", lambda e: e.activation(out=scr[:, tg * 384:(tg + 1) * 384], in_=PA[pa][:, 0:384],
                                                            func=AF.Copy), reads=[f"PA{pa}"], writes=["scrA"])
                    return f
                if sweep == 2:
                    for tg in range(3):
                        proj_fm(ws, 0, 128, tg * 384, 384, pre_evac(tg))
                    conv_silu(None, l, h, qT[:])
                for tg in range(3):
                    proj_fm(ws, kc0, 128, tg * 384, 384, pre_evac(tg))
                conv_silu(None, l, 4 + h, kT[:])
                for i in range(NTB):
                    def vev(pa, i=i):
                        P.op("act", lambda e: e.activation(out=vaug[:, i, 0:256], in_=PA[pa][:, 0:256], func=AF.Copy),
                             reads=[f"PA{pa}"], writes=["vaug"])
                    proj_tm(ws, 256, 256, i + 1, vev)
                if sweep == 2:
                    for i in range(NTB):
                        def oev(pa, i=i):
                            P.op("act", lambda e: e.activation(out=osig[:, i, :], in_=PA[pa][:, 0:256], func=AF.Sigmoid),
                                 reads=[f"PA{pa}"], writes=["osig"])
                        proj_tm(ws2, 0, 256, i + 1, oev)
                for i in range(NTB):
                    pt = next_pt()
                    col = i * 4 + h
                    P.op("pe", lambda e, i=i, pt=pt: e.transpose(out=PT[pt][:, 0:128], in_=kT[:, i * 128:(i + 1) * 128],
                                                                  identity=ident[:]),
                         reads=["qk", "ident"], writes=[f"PT{pt}"])
                    P.op("dve", lambda e, i=i, pt=pt, col=col: e.tensor_scalar(
                        out=ktok[:, i, :], in0=PT[pt][:, 0:128], scalar1=ksc[:, col:col + 1], scalar2=None, op0=ALU.mult),
                        reads=[f"PT{pt}", "ksc"], writes=["ktok"])
                def updU(i):
                    pu = 2 + i % 2
                    P.op("pe", lambda e: e.matmul(PA[pu][:, 0:257], lhsT=ktok[:, i, :], rhs=vaug[:, i, :], start=True, stop=True),
                         reads=["ktok", "vaug"], writes=[f"PA{pu}"])

                def updC(i, last):
                    col = i * 4 + h
                    pu = 2 + i % 2
                    P.op("dve", lambda e: e.scalar_tensor_tensor(out=Cst[:, h, 0:257], in0=Cst[:, h, 0:257],
                                                                 scalar=ebL[:, col:col + 1], in1=PA[pu][:, 0:257],
                                                                 op0=ALU.mult, op1=ALU.add),
                         reads=[f"PA{pu}", "ebL", "Cst"], writes=["Cst"])
                    if sweep == 1:
                        P.op("dve", lambda e: e.tensor_scalar(out=Cst[:, h, 257:258], in0=Cst[:, h, 257:258],
                                                               scalar1=ebL[:, col:col + 1], scalar2=None, op0=ALU.mult),
                             reads=["ebL", "Cst"], writes=["Cst"])
                    elif not last:
                        v = (i + 1) % 2
                        P.op("act", lambda e: e.activation(out=Cb2[v][:], in_=Cst[:, h, 0:257], func=AF.Copy),
                             reads=["Cst"], writes=[f"Cb2{v}"])

                def front(i):
                    col = i * 4 + h
                    u = i % 2
                    pb = (0, 4)[u]
                    tsl = slice(i * 128, (i + 1) * 128)
                    P.op("dve", lambda e: e.tensor_scalar(out=Lt[u][:], in0=onesf[:], scalar1=spt[:, col:col + 1],
                                                           scalar2=None, op0=ALU.mult),
                         reads=["onesf", "spt"], writes=[f"Lt{u}"])
                    P.op("pe", lambda e: e.matmul(PA[pb][:, 128:256], lhsT=kT[:, tsl], rhs=qT[:, tsl], start=True, stop=True),
                         reads=["qk"], writes=[f"PA{pb}"])
                    P.op("pe", lambda e: e.matmul(PA[pb][:, 0:128], lhsT=Lt[u][:], rhs=trif[:], start=True, stop=True),
                         reads=[f"Lt{u}", "trif"], writes=[f"PA{pb}"])
                    P.op("act", lambda e: e.activation(out=Et[u][:], in_=PA[pb][:, 0:128], func=AF.Exp, scale=-1.0),
                         reads=[f"PA{pb}"], writes=[f"Et{u}"])
                    P.op("dve", lambda e: e.tensor_tensor(out=EM[u][:], in0=Et[u][:], in1=maskS[:], op=ALU.mult),
                         reads=[f"Et{u}", "maskS"], writes=[f"EM{u}"])
                    P.op("dve", lambda e: e.scalar_tensor_tensor(out=qp[u][:], in0=qT[:, tsl], scalar=QSCALE,
                                                                 in1=Et[u][:], op0=ALU.mult, op1=ALU.mult),
                         reads=["qk", f"Et{u}"], writes=[f"qp{u}"])
                    P.op("dve", lambda e: e.scalar_tensor_tensor(out=Sb[u][:], in0=PA[pb][:, 128:256],
                                                                 scalar=ea[:, col:col + 1], in1=EM[u][:],
                                                                 op0=ALU.mult, op1=ALU.mult),
                         reads=[f"PA{pb}", "ea", f"EM{u}"], writes=[f"Sb{u}"])

                def numer(i):
                    u = i % 2
                    pn = (1, 5)[u]
                    P.op("pe", lambda e: e.matmul(PA[pn][:, 0:257], lhsT=Sb[u][:], rhs=vaug[:, i, :], start=True, stop=False),
                         reads=[f"Sb{u}", "vaug"], writes=[f"PA{pn}"])
                    P.op("pe", lambda e: e.matmul(PA[pn][:, 0:257], lhsT=qp[u][:], rhs=Cb2[u][:], start=False, stop=True),
                         reads=[f"qp{u}", f"Cb2{u}"], writes=[f"PA{pn}"])

                def back(i):
                    u = i % 2
                    pn = (1, 5)[u]
                    tsl = slice(i * 128, (i + 1) * 128)
                    r_ = nrm[:, u, 0:1]
                    s2 = nrm[:, u, 1:2]
                    a_ = nrm[:, u, 2:3]
                    P.op("dve", lambda e: e.tensor_scalar(out=a_, in0=PA[pn][:, 256:257], scalar1=-1.0, scalar2=1.0,
                                                           op0=ALU.mult, op1=ALU.max), reads=[f"PA{pn}"], writes=[f"nrm{u}"])
                    P.op("dve", lambda e: e.scalar_tensor_tensor(out=r_, in0=PA[pn][:, 256:257], scalar=1.0, in1=a_,
                                                                 op0=ALU.max, op1=ALU.max),
                         reads=[f"PA{pn}", f"nrm{u}"], writes=[f"nrm{u}"])
                    P.op("dve", lambda e: e.reciprocal(out=r_, in_=r_), reads=[f"nrm{u}"], writes=[f"nrm{u}"])
                    P.op("act", lambda e: e.activation(out=junk[:, 0:256], in_=PA[pn][:, 0:256], func=AF.Square,
                                                        scale=r_, accum_out=s2),
                         reads=[f"PA{pn}", f"nrm{u}"], writes=["junk", f"nrmb{u}"])
                    P.op("act", lambda e: e.activation(out=s2, in_=s2, func=AF.Ln, scale=1.0 / 256, bias=EPS),
                         reads=[f"nrmb{u}"], writes=[f"nrmb{u}"])
                    P.op("act", lambda e: e.activation(out=s2, in_=s2, func=AF.Exp, scale=-0.5),
                         reads=[f"nrmb{u}"], writes=[f"nrmb{u}"])
                    P.op("dve", lambda e: e.tensor_tensor(out=s2, in0=s2, in1=r_, op=ALU.mult),
                         reads=[f"nrmb{u}", f"nrm{u}"], writes=[f"nrmb{u}"])
                    P.op("dve", lambda e: e.scalar_tensor_tensor(out=t1[u][:], in0=PA[pn][:, 0:256], scalar=s2,
                                                                 in1=gm[:, h * 256:(h + 1) * 256],
                                                                 op0=ALU.mult, op1=ALU.mult),
                         reads=[f"PA{pn}", f"nrmb{u}", "gm"], writes=[f"t1{u}"])
                    P.op("dve", lambda e: e.tensor_tensor(out=hmt[u][:], in0=t1[u][:], in1=osig[:, i, :], op=ALU.mult),
                         reads=[f"t1{u}", "osig"], writes=[f"hmt{u}"])
                    pt = next_pt()
                    for c in range(2):
                        P.op("pe", lambda e, c=c: e.transpose(out=PT[pt][:, c * 128:(c + 1) * 128],
                                                              in_=hmt[u][:, c * 128:(c + 1) * 128], identity=ident[:]),
                             reads=[f"hmt{u}", "ident"], writes=[f"PT{pt}"])
                    P.op("act", lambda e: e.activation(
                        out=hmT[:, 2 * h:2 * h + 2, tsl],
                        in_=PT[pt][:, 0:256].rearrange("p (c t) -> p c t", c=2), func=AF.Copy),
                        reads=[f"PT{pt}"], writes=["hmT"])

                if sweep == 1:
                    for i in range(NTB):
                        updU(i)
                        updC(i, i == NTB - 1)
                else:
                    P.op("act", lambda e: e.activation(out=Cb2[0][:], in_=Cst[:, h, 0:257], func=AF.Copy),
                         reads=["Cst"], writes=["Cb20"])
                    front(0)
                    for i in range(NTB):
                        if i + 1 < NTB:
                            front(i + 1)
                        updU(i)
                        numer(i)
                        updC(i, i == NTB - 1)
                        back(i)
                P.barrier()

        def phase_attn(l, b, g):
            with ExitStack() as ph:
                def psb(name, shape, dt=F32):
                    uid[0] += 1
                    return ph.enter_context(nc.sbuf_tensor(f"ph_{name}_{uid[0]}", list(shape), dt))
                qTa = psb("qTa", [128, 2, TB], BF16)
                kTa = psb("kTa", [128, 9 * 128], BF16)
                va = psb("va", [128, 9, 65], BF16)
                hat = [psb(f"hat{i}", [128, 4, 64], BF16) for i in range(2)]
                ND = 4
                lg = [psb(f"lg{i}", [128, 256]) for i in range(ND)]
                pTt = [psb(f"pTt{i}", [128, 2, 128], BF16) for i in range(ND)]
                dn = psb("dn", [128, 2, 4])
                ws = load_w([wcols(w_in_d[l], OFF_AQ + g * 256, 256, 0),
                             wcols(w_in_d[l], OFF_AK + g * 64, 64, 256),
                             wcols(w_in_d[l], OFF_AK + g * 64, 64, 320),
                             wcols(w_in_d[l], OFF_AV + g * 64, 64, 384)])
                for c in range(2):
                    for tg in range(2):
                        def qev(pa, c=c, tg=tg):
                            P.op("act", lambda e: e.activation(out=qTa[:, c, tg * 512:(tg + 1) * 512], in_=PA[pa][:, 0:512],
                                                                func=AF.Copy), reads=[f"PA{pa}"], writes=["qTa"])
                        proj_fm(ws, c * 128, 128, 128 + tg * 512, 512, qev)
                for tg in range(3):
                    def kev(pa, tg=tg):
                        P.op("dve", lambda e: e.tensor_copy(out=kTa[:, tg * 384:(tg + 1) * 384], in_=PA[pa][:, 0:384]),
                             reads=[f"PA{pa}"], writes=["kTa"])
                    proj_fm(ws, 256, 128, tg * 384, 384, kev)
                P.op("dve", lambda e: e.memset(va[:, :, 64:65], 1.0), writes=["va"])
                if b == 0:
                    P.op("dve", lambda e: e.tensor_copy(out=va[:, 0, 64:65], in_=flags[:, 16:17]),
                         reads=["flags"], writes=["va"])
                for s_ in range(9):
                    def vev(pa, s_=s_):
                        P.op("act", lambda e: e.activation(out=va[:, s_, 0:64], in_=PA[pa][:, 0:64], func=AF.Copy),
                             reads=[f"PA{pa}"], writes=["va"])
                    proj_tm(ws, 384, 64, s_, vev)
                items = [(i, j) for i in range(NTB) for j in range(4)]

                def stage_a(n):
                    i, j = items[n]
                    c = j // 2
                    p0 = (j % 2) * 64
                    pa = 2 + n % ND
                    tsl = slice(i * 128, (i + 1) * 128)
                    for blk in range(2):
                        P.op("pe", lambda e: e.matmul(
                            PA[pa][:, blk * 128:(blk + 1) * 128],
                            lhsT=kTa[p0:p0 + 64, (i + blk) * 128:(i + blk + 1) * 128],
                            rhs=qTa[p0:p0 + 64, c, tsl], start=True, stop=True),
                            reads=["kTa", "qTa"], writes=[f"PA{pa}"])

                def stage_b(n):
                    i, j = items[n]
                    hq = 4 * g + j
                    u = n % ND
                    pa = 2 + u
                    po = i % 2
                    P.op("dve", lambda e: e.scalar_tensor_tensor(
                        out=lg[u][:], in0=PA[pa][:, 0:256], scalar=0.125,
                        in1=biasm[:, hq, :, :].rearrange("p b q -> p (b q)"), op0=ALU.mult, op1=ALU.add),
                        reads=[f"PA{pa}", "biasm"], writes=[f"lg{u}"])
                    P.op("act", lambda e: e.activation(out=pTt[u][:].rearrange("p b q -> p (b q)"), in_=lg[u][:],
                                                        func=AF.Exp), reads=[f"lg{u}"], writes=[f"pTt{u}"])
                    P.op("pe", lambda e: e.matmul(PA[po][:, j * 65:(j + 1) * 65], lhsT=pTt[u][:, 0, :], rhs=va[:, i, :],
                                                  start=True, stop=False),
                         reads=[f"pTt{u}", "va"], writes=[f"PA{po}"])
                    P.op("pe", lambda e: e.matmul(PA[po][:, j * 65:(j + 1) * 65], lhsT=pTt[u][:, 1, :], rhs=va[:, i + 1, :],
                                                  start=False, stop=True),
                         reads=[f"pTt{u}", "va"], writes=[f"PA{po}"])
                    if j == 3:
                        ov = PA[po][:, 0:260].rearrange("p (h c) -> p h c", c=65)
                        dv = dn[:, po, :]
                        P.op("dve", lambda e: e.tensor_tensor(out=dv, in0=ov[:, :, 64], in1=esink[:, 4 * g:4 * g + 4], op=ALU.add),
                             reads=[f"PA{po}", "esink"], writes=[f"dn{po}"])
                        P.op("dve", lambda e: e.reciprocal(out=dv, in_=dv), reads=[f"dn{po}"], writes=[f"dn{po}"])
                        for jj in range(4):
                            P.op("act", lambda e, jj=jj: e.activation(out=hat[po][:, jj, :], in_=PA[po][:, jj * 65:jj * 65 + 64],
                                                                       func=AF.Copy, scale=dn[:, po, jj:jj + 1]),
                                 reads=[f"PA{po}", f"dn{po}"], writes=[f"hat{po}"])
                        pt = next_pt()
                        hv = hat[po][:].rearrange("p h d -> p (h d)")
                        for c in range(2):
                            P.op("pe", lambda e, c=c: e.transpose(out=PT[pt][:, c * 128:(c + 1) * 128],
                                                                  in_=hv[:, c * 128:(c + 1) * 128], identity=ident[:]),
                                 reads=[f"hat{po}", "ident"], writes=[f"PT{pt}"])
                        P.op("dve", lambda e: e.tensor_copy(
                            out=haT[:, 2 * g:2 * g + 2, i * 128:(i + 1) * 128],
                            in_=PT[pt][:, 0:256].rearrange("p (c t) -> p c t", c=2)),
                            reads=[f"PT{pt}"], writes=["haT"])

                DEPTH = ND - 1
                for n in range(min(DEPTH, len(items))):
                    stage_a(n)
                for n in range(len(items)):
                    if n + DEPTH < len(items):
                        stage_a(n + DEPTH)
                    stage_b(n)
                P.barrier()

        def phase_tail(l, b):
            with ExitStack() as ph:
                def psb(name, shape, dt=F32):
                    uid[0] += 1
                    return ph.enter_context(nc.sbuf_tensor(f"ph_{name}_{uid[0]}", list(shape), dt))
                yT = psb("yT", [128, 8, TB], BF16)
                sga = [psb(f"sga{i}", [128, 512]) for i in range(2)]
                sgb = [psb(f"sgb{i}", [128, 512]) for i in range(2)]
                it = 0
                for ncx in range(8):
                    n0 = ncx * 128
                    ws = load_w([wcols(w_m_d[l], n0, 128, 0), wcols(w_a_d[l], n0, 128, 128),
                                 wcols(w_in_d[l], OFF_GA + n0, 128, 256), wcols(w_in_d[l], OFF_GB + n0, 128, 384)])
                    for tg in range(2):
                        t0 = tg * 512
                        u = it % 2
                        it += 1
                        srcs = {"ga": (hT, 256, 128, HT_BLK), "gb": (hT, 384, 128, HT_BLK),
                                "m": (hmT, 0, 0, ["hmT"]), "a": (haT, 128, 0, ["haT"])}
                        bank = {}
                        for key in ("ga", "gb", "m", "a"):
                            src, wc, toff, rd = srcs[key]
                            pa = next_pa(6)
                            bank[key] = pa
                            for k in range(8):
                                P.op("pe", lambda e, k=k: e.matmul(
                                    PA[pa][:, 0:512], lhsT=wb[ws][:, k, wc:wc + 128],
                                    rhs=src[:, k, toff + t0:toff + t0 + 512], start=(k == 0), stop=(k == 7)),
                                    reads=[f"wb{ws}"] + rd, writes=[f"PA{pa}"])
                        P.op("act", lambda e: e.activation(out=sga[u][:], in_=PA[bank["ga"]][:, 0:512], func=AF.Sigmoid),
                             reads=[f"PA{bank['ga']}"], writes=[f"sga{u}"])
                        P.op("act", lambda e: e.activation(out=sgb[u][:], in_=PA[bank["gb"]][:, 0:512], func=AF.Sigmoid),
                             reads=[f"PA{bank['gb']}"], writes=[f"sgb{u}"])
                        P.op("dve", lambda e: e.tensor_tensor(out=sga[u][:], in0=PA[bank["m"]][:, 0:512], in1=sga[u][:], op=ALU.mult),
                             reads=[f"PA{bank['m']}", f"sga{u}"], writes=[f"sga{u}"])
                        P.op("dve", lambda e: e.tensor_tensor(out=sgb[u][:], in0=PA[bank["a"]][:, 0:512], in1=sgb[u][:], op=ALU.mult),
                             reads=[f"PA{bank['a']}", f"sgb{u}"], writes=[f"sgb{u}"])
                        P.op("dve", lambda e: e.tensor_tensor(out=yT[:, ncx, t0:t0 + 512], in0=sga[u][:], in1=sgb[u][:],
                                                              op=ALU.add),
                             reads=[f"sga{u}", f"sgb{u}"], writes=["yT"])
                for nh in range(2):
                    ws = load_w([wcols(w_o_d[l], nh * 512, 512, 0)])
                    for i in range(NTB):
                        gt_ = b * NTB + i
                        pa = next_pa(6)
                        for k in range(8):
                            P.op("pe", lambda e, k=k: e.matmul(PA[pa][:, 0:512], lhsT=yT[:, k, i * 128:(i + 1) * 128],
                                                               rhs=wb[ws][:, k, 0:512], start=(k == 0), stop=(k == 7)),
                                 reads=[f"wb{ws}", "yT"], writes=[f"PA{pa}"])
                        xs = x[:, gt_, nh * 512:(nh + 1) * 512]
                        P.op("dve", lambda e: e.tensor_tensor(out=xs, in0=PA[pa][:, 0:512], in1=xs, op=ALU.add),
                             reads=[f"PA{pa}", f"x{gt_}"], writes=[f"x{gt_}"])
                P.barrier()

        def phase_mlp(l, b):
            with ExitStack() as ph:
                def psb(name, shape, dt=F32):
                    uid[0] += 1
                    return ph.enter_context(nc.sbuf_tensor(f"ph_{name}_{uid[0]}", list(shape), dt))
                uT = [psb(f"uT{i}", [128, 4, TB], BF16) for i in range(2)]
                rt = [psb(f"rt{i}", [128, 512]) for i in range(2)]
                phase_norm(l, b, 2 * l + 1, with_halo=False)
                for fg in range(8):
                    wu = load_w([wcols(w_up_d[l], fg * 512, 512, 0)])
                    wd = load_w([(w_dn_d[l][fg * 512:(fg + 1) * 512, :].rearrange("(k p) n -> p k n", p=128),
                                  lambda w: w[:].rearrange("p k n -> p (k n)").rearrange("p (k n) -> p k n", k=4))])
                    wdv = wb[wd][:].rearrange("p k n -> p (k n)").rearrange("p (k n) -> p k n", k=4)
                    u = uT[fg % 2]
                    un = f"uT{fg % 2}"
                    for fc in range(4):
                        for tg in range(2):
                            r_ = rt[(fc * 2 + tg) % 2]
                            rn = f"rt{(fc * 2 + tg) % 2}"
                            def uev(pa, fc=fc, tg=tg, r_=r_, rn=rn, u=u, un=un):
                                P.op("act", lambda e: e.activation(out=r_[:], in_=PA[pa][:, 0:512], func=AF.Relu),
                                     reads=[f"PA{pa}"], writes=[rn])
                                P.op("act", lambda e: e.activation(out=u[:, fc, tg * 512:(tg + 1) * 512], in_=r_[:], func=AF.Square),
                                     reads=[rn], writes=[un])
                            proj_fm(wu, fc * 128, 128, 128 + tg * 512, 512, uev)
                    for i in range(NTB):
                        gt_ = b * NTB + i
                        for nh in range(2):
                            pa = next_pa()
                            for fc in range(4):
                                P.op("pe", lambda e, fc=fc, i=i, nh=nh, pa=pa: e.matmul(
                                    PA[pa][:, 0:512], lhsT=u[:, fc, i * 128:(i + 1) * 128],
                                    rhs=wdv[:, fc, nh * 512:(nh + 1) * 512], start=(fc == 0), stop=(fc == 3)),
                                    reads=[f"wb{wd}", un], writes=[f"PA{pa}"])
                            xs = x[:, gt_, nh * 512:(nh + 1) * 512]
                            P.op("dve", lambda e, xs=xs, pa=pa: e.tensor_tensor(out=xs, in0=PA[pa][:, 0:512], in1=xs, op=ALU.add),
                                 reads=[f"PA{pa}", f"x{gt_}"], writes=[f"x{gt_}"])
                P.barrier()

        def phase_exchange(l):
            if mode == "main":
                gsrc = gath_in
            else:
                gsrc = gath[l].ap()
                P.dma("pool", lambda e: e.dma_start(out=bounce[l].ap(), in_=Cst[:].rearrange("p h c -> p (h c)")),
                      f"bnc{l}", reads=["Cst"], writes=[f"bounce{l}"])
                P.cc(lambda e: e.collective_compute("AllGather", ALU.bypass, replica_groups=[list(range(NCORES))],
                                                    ins=[bounce[l].ap().opt()], outs=[gath[l].ap().opt()]),
                     f"ag{l}", reads=[f"bounce{l}"], writes=[f"gath{l}"])
            P.op("dve", lambda e: e.memset(Cst[:], 0.0), reads=["Cst"], writes=["Cst"])
            stg = scr[:, 0:4 * CW]
            for r in range(NCORES):
                P.dma("sp", lambda e, r=r: e.dma_start(out=stg, in_=gsrc[r * 128:(r + 1) * 128, :]),
                      "scr", reads=[f"gath{l}", "scrA", "scrB"], writes=["scr", "scrA", "scrB"])
                s_r = flags[:, r:r + 1]
                for h in range(4):
                    dcol = scr[:, h * CW + 257:h * CW + 258]
                    dd = ss[:, 5:6]
                    P.op("dve", lambda e, dcol=dcol, s_r=s_r, dd=dd: e.tensor_scalar(
                        out=dd, in0=dcol, scalar1=-1.0, scalar2=s_r, op0=ALU.add, op1=ALU.mult),
                        reads=["scr", "flags"], writes=["ss5"])
                    P.op("dve", lambda e, dd=dd: e.tensor_scalar(out=dd, in0=dd, scalar1=1.0, scalar2=None, op0=ALU.add),
                         reads=["ss5"], writes=["ss5"])
                    P.op("dve", lambda e, h=h, dd=dd: e.tensor_scalar(out=Cst[:, h, 0:257], in0=Cst[:, h, 0:257],
                                                                      scalar1=dd, scalar2=None, op0=ALU.mult),
                         reads=["ss5", "Cst"], writes=["Cst"])
                    P.op("dve", lambda e, h=h, s_r=s_r: e.scalar_tensor_tensor(
                        out=Cst[:, h, 0:257], in0=scr[:, h * CW:h * CW + 257], scalar=s_r, in1=Cst[:, h, 0:257],
                        op0=ALU.mult, op1=ALU.add), reads=["scr", "flags", "Cst"], writes=["Cst"])
            P.barrier()

        def phase_exchange_halo():
            P.dma("pool", lambda e: e.dma_start(out=bounce_h.ap(), in_=x[:, NTL - 1, :]),
                  "bnch", reads=[f"x{NTL - 1}"], writes=["bounce_h"])
            P.cc(lambda e: e.collective_compute("AllGather", ALU.bypass, replica_groups=[list(range(NCORES))],
                                                ins=[bounce_h.ap().opt()], outs=[gath_h.ap().opt()]),
                 "agh", reads=["bounce_h"], writes=["gath_h"])
            P.op("dve", lambda e: e.memset(xhalo[:], 0.0), reads=["xhalo"], writes=["xhalo"])
            stg = scr[:, 0:D]
            for r in range(NCORES):
                P.dma("sp", lambda e, r=r: e.dma_start(out=stg, in_=gath_h.ap()[r * 128:(r + 1) * 128, :]),
                      "scr", reads=["gath_h", "scrA", "scrB"], writes=["scr", "scrA", "scrB"])
                p_r = flags[:, 8 + r:9 + r]
                P.op("dve", lambda e, p_r=p_r: e.scalar_tensor_tensor(
                    out=xhalo[:], in0=stg, scalar=p_r, in1=xhalo[:], op0=ALU.mult, op1=ALU.add),
                    reads=["scr", "flags", "xhalo"], writes=["xhalo"])
            P.barrier()

        steps = []
        for l in LAYERS:
            def prolog(l=l):
                cload(gm[:], gm_d[:, l, :], "gm")
                P.op("act", lambda e, l=l: e.activation(out=esink[:], in_=sinks[:, l, :], func=AF.Exp),
                     reads=["sinks"], writes=["esink"])
                if l > l0 and not no_cc:
                    phase_exchange_halo()
                P.op("dve", lambda e: e.memset(Cst[:], 0.0), reads=["Cst"], writes=["Cst"])
                P.op("dve", lambda e: e.memset(Cst[:, :, 257:258], 1.0), reads=["Cst"], writes=["Cst"])
            steps.append(prolog)
            if mode != "main":
                for b in range(NBLK):
                    steps.append(lambda l=l, b=b: phase_norm(l, b, 2 * l))
                    steps.append(lambda l=l, b=b: phase_gates(l, b))
                    for h in range(4):
                        steps.append(lambda l=l, b=b, h=h: phase_mlstm_head(l, b, h, 1))
            if mode == "sum":
                continue
            def exch(l=l):
                if no_cc:
                    P.op("dve", lambda e: e.memset(Cst[:], 0.0), reads=["Cst"], writes=["Cst"])
                else:
                    phase_exchange(l)
            steps.append(exch)
            for b in range(NBLK):
                steps.append(lambda l=l, b=b: phase_norm(l, b, 2 * l))
                steps.append(lambda l=l, b=b: phase_gates(l, b))
                for h in range(4):
                    steps.append(lambda l=l, b=b, h=h: phase_mlstm_head(l, b, h, 2))
                for g in range(4):
                    steps.append(lambda l=l, b=b, g=g: phase_attn(l, b, g))
                steps.append(lambda l=l, b=b: phase_tail(l, b))
                steps.append(lambda l=l, b=b: phase_mlp(l, b))
        for si, stp in enumerate(steps):
            if nsteps is not None and si >= nsteps:
                break
            stp()
        otoks = []
        if mode == "sum":
            tcs = P.dma("sp", lambda e: e.dma_start(out=cst_out, in_=Cst[:].rearrange("p h c -> p (h c)")), "out", reads=["Cst"])
            P.final_wait("sp", [tcs])
            P.emit(st)
            return nc
        if dbg:
            otoks.append(P.dma("sp", lambda e: e.dma_start(out=d_hT, in_=hT[:].rearrange("p c t -> p (c t)")), "out", reads=HT_ALL))
            otoks.append(P.dma("sp", lambda e: e.dma_start(out=d_hmT, in_=hmT[:].rearrange("p c t -> p (c t)")), "out", reads=["hmT"]))
            otoks.append(P.dma("sp", lambda e: e.dma_start(out=d_haT, in_=haT[:].rearrange("p c t -> p (c t)")), "out", reads=["haT"]))
            otoks.append(P.dma("sp", lambda e: e.dma_start(out=d_misc[:, 0:4 * CW], in_=Cst[:].rearrange("p h c -> p (h c)")), "out", reads=["Cst"]))
            for qi, (buf, nm) in enumerate([(gsum[:].rearrange("p i c -> p (i c)"), "gsum"), (spt[:], "spt"), (a_t[:], "a_t"), (ea[:], "ea"), (ksc[:], "ksc"), (ebL[:], "ebL")]):
                w_ = 64 if nm == "gsum" else 32
                o0 = 4 * CW + (0 if qi == 0 else 64 + (qi - 1) * 32)
                otoks.append(P.dma("sp", lambda e, buf=buf, o0=o0, w_=w_: e.dma_start(out=d_misc[:, o0:o0 + w_], in_=buf), "out", reads=[nm]))
        for i in range(NTL):
            xi = x[:, i, :]
            P.op("act", lambda e, xi=xi: e.activation(out=junk[:], in_=xi, func=AF.Square, accum_out=ss[:, 0:1]),
                 reads=[f"x{i}"], writes=["junk", "ss"])
            rstd_from_ss(0, 1.0 / D)
            ob = scr[:, (i % 2) * D:(i % 2 + 1) * D]
            on = f"ob{i % 2}"
            if not final:
                P.op("dve", lambda e, xi=xi, ob=ob: e.tensor_copy(out=ob, in_=xi), reads=[f"x{i}", "scr", "scrA", "scrB"], writes=[on])
            else:
                P.op("dve", lambda e, xi=xi, ob=ob: e.scalar_tensor_tensor(out=ob, in0=xi, scalar=ss[:, 0:1], in1=gfin[:],
                                                                           op0=ALU.mult, op1=ALU.mult),
                     reads=[f"x{i}", "ss", "gfin", "scr", "scrA", "scrB"], writes=[on])
            otoks.append(P.dma("sp", lambda e, i=i, ob=ob: e.dma_start(out=out_d[i * 128:(i + 1) * 128, :], in_=ob),
                               "out", reads=[on]))
        P.final_wait("sp", [otoks[-1]])
        P.emit(st)
    return nc


def _t5_bucket(n):
    n_buckets, max_distance = 32, 128
    max_exact = n_buckets // 2
    n = np.maximum(n, 0)
    large = max_exact + (np.log(np.maximum(n, 1) / max_exact) / np.log(max_distance / max_exact)
                         * (n_buckets - max_exact)).astype(np.int32)
    large = np.minimum(large, n_buckets - 1)
    return np.where(n < max_exact, n, large).astype(np.int32)


def _host_layout(inputs, layers=(0, 1), xsrc=None):
    f = lambda a: np.ascontiguousarray(np.asarray(a), dtype=np.float32)
    x = f(inputs["x"]) if xsrc is None else xsrc
    rep = lambda v: np.ascontiguousarray(np.broadcast_to(v, (128,) + v.shape))
    common = {}
    for l in layers:
        for nm in ("w_in", "w_branch_m", "w_branch_a", "w_out", "w_up", "w_down"):
            common[f"{nm}{l}"] = np.ascontiguousarray(f(inputs[nm])[l])
    gall = np.stack([f(inputs["norm_mix_g"])[0], f(inputs["norm_mlp_g"])[0],
                     f(inputs["norm_mix_g"])[1], f(inputs["norm_mlp_g"])[1]], 0)
    common["gcols"] = np.ascontiguousarray(gall.reshape(4, 8, 128).transpose(2, 0, 1))
    common["gfin"] = rep(f(inputs["final_norm_g"]))
    cw = f(inputs["conv_w"])
    common["convw"] = np.ascontiguousarray(cw.reshape(2, 4, 8, 128).transpose(3, 0, 2, 1))
    common["convb"] = np.ascontiguousarray(f(inputs["conv_b"]).reshape(2, 8, 128).transpose(2, 0, 1))
    gb = np.concatenate([f(inputs["b_igate"]), f(inputs["b_fgate"])], axis=1)
    common["gateb"] = rep(np.ascontiguousarray(np.tile(gb[:, None, :], (1, 8, 1)).reshape(2, 64)))
    common["gm"] = rep(f(inputs["mlstm_norm_g"]))
    common["sinks"] = rep(f(inputs["attn_sinks"]))
    kk = np.arange(128)[:, None]
    qq = np.arange(128)[None, :]
    rb = f(inputs["rel_bias"])
    biasT = np.zeros((128, 16, 2, 128), np.float32)
    maskT = np.zeros((128, 2, 128), np.float32)
    for blk in range(2):
        dist = qq - kk + (128 if blk == 0 else 0)
        valid = (dist >= 0) & (dist < 128)
        bt = rb[_t5_bucket(dist)]
        biasT[:, :, blk, :] = np.where(valid[:, None, :], bt.transpose(0, 2, 1), 0.0)
        maskT[:, blk, :] = np.where(valid, 0.0, NEG)
    common["biasT"] = biasT
    common["maskT"] = maskT
    in_maps = []
    for r in range(NCORES):
        sq, j = r // 4, r % 4
        m = dict(common)
        m["x"] = np.ascontiguousarray(x[sq, j * TL:(j + 1) * TL, :])
        m["xh"] = (np.ascontiguousarray(x[sq, j * TL - 128:j * TL, :]) if j > 0 else np.zeros((128, D), np.float32))
        fl = np.zeros((128, 24), np.float32)
        for rp in range(NCORES):
            if rp // 4 == sq and rp < r:
                fl[:, rp] = 1.0
        if j > 0:
            fl[:, 8 + r - 1] = 1.0
            fl[:, 16] = 1.0
        m["flags"] = fl
        in_maps.append(m)
    return in_maps


def run(inputs, depth=2, dbg=False, nsteps=None, no_cc=False, ncores=NCORES, l0=0, final=None, xsrc=None,
        mode="fused", gath_in=None):
    in_maps = _host_layout(inputs, layers=list(range(l0, l0 + depth)), xsrc=xsrc)[:ncores]
    if gath_in is not None:
        for m in in_maps:
            m["gath_in"] = gath_in
    nc = build_program(depth=depth, dbg=dbg, nsteps=nsteps, no_cc=no_cc, l0=l0, final=final, mode=mode)
    res = run_bass_kernel_spmd(nc, in_maps, core_ids=list(range(ncores)))
    if mode == "sum":
        return np.concatenate([res.results[r]["cst_out"] for r in range(ncores)], axis=0)
    global LAST_RES
    LAST_RES = res
    out = np.zeros((2, SEQ, D), np.float32)
    for r in range(ncores):
        out[r // 4, (r % 4) * TL:(r % 4 + 1) * TL, :] = res.results[r]["out"]
    return out


def kernel(**inputs):
    xcur = None
    for l in range(2):
        g = run(inputs, depth=1, l0=l, mode="sum", xsrc=xcur)
        xcur = run(inputs, depth=1, l0=l, mode="main", final=(l == 1), xsrc=xcur, gath_in=g)
    return xcur
```
